# Optimizing a Trainium2 kernel written in Bass

```python
import math
import jax, jax.numpy as jnp
from jax import lax
import numpy as np

D_MODEL = 2048
BATCH = 32
SEQ = 256
DEPTH = 4
DEC_BATCH = 8
DEC_SEQ = 4096
PAST_LEN = 256

GRID_W = 64
EPS = 1e-6
POOL_W = 512
POOL_WINDOWS = (2, 4, 8, 16)
POOL_GROUPS = len(POOL_WINDOWS)
POOL_GW = POOL_W // POOL_GROUPS
N_HEADS = 8
N_KV = 2
HEAD_DIM = 128
GQA_G = N_HEADS // N_KV
ATTN_W = N_HEADS * HEAD_DIM
KV_W = N_KV * HEAD_DIM
WINDOW = 128
BLOCK = 128
ROPE_BASE = 10000.0
LRU_W = 512
LRU_BLOCKS = 4
LRU_BW = LRU_W // LRU_BLOCKS
CONV_W = 4
CONV_LEFT = 2
LRU_C = 8.0
N_BRANCH = 3
IN_SPLITS = (POOL_W, POOL_W, ATTN_W, KV_W, KV_W, ATTN_W, LRU_W, LRU_W, N_BRANCH * D_MODEL)
IN_W = sum(IN_SPLITS)
IN_OFFSETS = tuple(sum(IN_SPLITS[:i + 1]) for i in range(len(IN_SPLITS) - 1))

kernel_name = 'hybrid_diffusion_prefix_pool_swa_rglru_step'

F32 = jnp.float32


def rms_norm(x, g):
    xf = x.astype(F32)
    y = xf * lax.rsqrt(jnp.mean(xf * xf, axis=-1, keepdims=True) + EPS)
    return (y * g.astype(F32)).astype(x.dtype)


def pool_mixer(u, w_map, scale):
    B, T, _ = u.shape
    uf = u.astype(F32)
    csum = jnp.pad(jnp.cumsum(uf, axis=1), ((0, 0), (1, 0), (0, 0)))
    t = jnp.arange(T)
    groups = []
    for gi, w in enumerate(POOL_WINDOWS):
        sl = slice(gi * POOL_GW, (gi + 1) * POOL_GW)
        lo = jnp.clip(t - w // 2, 0, T)
        hi = jnp.clip(t + w // 2, 0, T)
        cs = csum[..., sl]
        mean = (cs[:, hi] - cs[:, lo]) / (hi - lo).astype(F32)[None, :, None]
        groups.append(mean - uf[..., sl])
    d = jnp.stack(groups, axis=2)
    y = jnp.einsum('btgc,gce->btge', d, w_map.astype(F32)).reshape(B, T, POOL_W)
    return (y * scale.astype(F32)).astype(u.dtype)


def axial_rope_tables(T):
    rows = T // GRID_W
    row = jnp.repeat(jnp.arange(rows, dtype=F32), GRID_W)
    col = jnp.tile(jnp.arange(GRID_W, dtype=F32), rows)
    n_freq = HEAD_DIM // 4
    inv = ROPE_BASE ** (-jnp.arange(n_freq, dtype=F32) / n_freq)
    ang_r = row[:, None] * inv[None]
    ang_c = col[:, None] * inv[None]
    return jnp.cos(ang_r), jnp.sin(ang_r), jnp.cos(ang_c), jnp.sin(ang_c)


def _rotate(x, cos, sin):
    n = x.shape[-1] // 2
    x1, x2 = x[..., :n], x[..., n:]
    return jnp.concatenate([x1 * cos - x2 * sin, x2 * cos + x1 * sin], axis=-1)


def apply_axial_rope(x, tables):
    cos_r, sin_r, cos_c, sin_c = tables
    extra = (1,) * (x.ndim - 3)
    ex = lambda a: a.reshape((1, a.shape[0]) + extra + (a.shape[1],))
    xf = x.astype(F32)
    half = HEAD_DIM // 2
    xr = _rotate(xf[..., :half], ex(cos_r), ex(sin_r))
    xc = _rotate(xf[..., half:], ex(cos_c), ex(sin_c))
    return jnp.concatenate([xr, xc], axis=-1).astype(x.dtype)


def context_attention(q, k, v, sink):
    B, T = q.shape[:2]
    nb = T // BLOCK
    scale = HEAD_DIM ** -0.5
    qb = jnp.moveaxis(q.reshape(B, nb, BLOCK, N_KV, GQA_G, HEAD_DIM), 1, 0)

    def one_block(qi):
        s = jnp.einsum('bqkgd,bskd->bkgqs', qi, k, preferred_element_type=F32) * scale
        sk = jnp.broadcast_to(sink.astype(F32)[None, :, :, None, None], s.shape[:-1] + (1,))
        p = jax.nn.softmax(jnp.concatenate([s, sk], axis=-1), axis=-1)[..., :-1]
        return jnp.einsum('bkgqs,bskd->bqkgd', p.astype(v.dtype), v)

    out = lax.map(one_block, qb)
    return jnp.moveaxis(out, 0, 1).reshape(B, T, ATTN_W)


def latent_attention(q, k, v, k_ctx, v_ctx, sink):
    B, T = q.shape[:2]
    nb = T // BLOCK
    span = BLOCK + 2 * WINDOW
    scale = HEAD_DIM ** -0.5
    pad = ((0, 0), (WINDOW, WINDOW), (0, 0), (0, 0))
    kp = jnp.pad(k, pad)
    vp = jnp.pad(v, pad)
    qb = jnp.moveaxis(q.reshape(B, nb, BLOCK, N_KV, GQA_G, HEAD_DIM), 1, 0)

    def one_block(args):
        qi, bi = args
        start = bi * BLOCK
        kw = lax.dynamic_slice_in_dim(kp, start, span, axis=1)
        vw = lax.dynamic_slice_in_dim(vp, start, span, axis=1)
        qpos = start + jnp.arange(BLOCK)
        kpos = start - WINDOW + jnp.arange(span)
        valid = (jnp.abs(qpos[:, None] - kpos[None, :]) <= WINDOW) & (kpos >= 0)[None, :] & (kpos < T)[None, :]
        s_w = jnp.einsum('bqkgd,bskd->bkgqs', qi, kw, preferred_element_type=F32) * scale
        s_w = jnp.where(valid, s_w, -jnp.inf)
        s_c = jnp.einsum('bqkgd,bskd->bkgqs', qi, k_ctx, preferred_element_type=F32) * scale
        sk = jnp.broadcast_to(sink.astype(F32)[None, :, :, None, None], s_w.shape[:-1] + (1,))
        p = jax.nn.softmax(jnp.concatenate([s_w, s_c, sk], axis=-1), axis=-1)
        p_w = p[..., :span].astype(v.dtype)
        p_c = p[..., span:-1].astype(v.dtype)
        return (jnp.einsum('bkgqs,bskd->bqkgd', p_w, vw)
                + jnp.einsum('bkgqs,bskd->bqkgd', p_c, v_ctx))

    out = lax.map(one_block, (qb, jnp.arange(nb)))
    return jnp.moveaxis(out, 0, 1).reshape(B, T, ATTN_W)


def _linear_scan(a, b, h0, reverse):
    idx = -1 if reverse else 0
    b = b.at[:, idx].add(a[:, idx] * h0)

    def combine(e1, e2):
        a1, b1 = e1
        a2, b2 = e2
        return a1 * a2, a2 * b1 + b2

    _, h = lax.associative_scan(combine, (a, b), reverse=reverse, axis=1)
    return h


def rglru_branch(u, conv_w, conv_b, gate_w, gate_b, lam, h0):
    B, T, _ = u.shape
    uf = u.astype(F32)
    up = jnp.pad(uf, ((0, 0), (CONV_LEFT, CONV_W - 1 - CONV_LEFT), (0, 0)))
    cw = conv_w.astype(F32)
    xc = conv_b.astype(F32) + up[:, 0:T] * cw[0]
    for j in range(1, CONV_W):
        xc = xc + up[:, j:j + T] * cw[j]
    g = jnp.einsum('btnc,dknce->dkbtne', xc.reshape(B, T, LRU_BLOCKS, LRU_BW), gate_w.astype(F32))
    g = g.reshape(2, 2, B, T, LRU_W) + gate_b.astype(F32)[:, :, None, None, :]
    r = jax.nn.sigmoid(g[:, 0])
    i = jax.nn.sigmoid(g[:, 1])
    log_a = -LRU_C * r * jax.nn.softplus(-lam.astype(F32))[:, None, None, :]
    a = jnp.exp(log_a)
    b = jnp.sqrt(jnp.maximum(-jnp.expm1(2.0 * log_a), 0.0)) * (i * xc[None])
    h0f = h0.astype(F32)
    h_f = _linear_scan(a[0], b[0], h0f[:, 0], reverse=False)
    h_b = _linear_scan(a[1], b[1], h0f[:, 1], reverse=True)
    return h_f, h_b


def trunk_layer(x, mod, P, ctx=None):
    B, T, _ = x.shape
    shift, scale, gate = jnp.split(mod, 3, axis=-1)
    h = rms_norm(x, P['g_pre']) * (1 + scale[:, None]) + shift[:, None]
    (u_pool, z_pool, q, k, v, z_attn, u_lru, z_lru, merge_logits) = jnp.split(
        h @ P['w_in'], IN_OFFSETS, axis=-1)
    y_pool = pool_mixer(u_pool, P['w_pool_map'], P['pool_scale']) * jax.nn.silu(z_pool)
    q = q.reshape(B, T, N_KV, GQA_G, HEAD_DIM)
    k = k.reshape(B, T, N_KV, HEAD_DIM)
    v = v.reshape(B, T, N_KV, HEAD_DIM)
    sink = P['attn_sink'].reshape(N_KV, GQA_G)
    if ctx is None:
        o_attn = context_attention(q, k, v, sink)
        h0 = jnp.zeros((B, 2, LRU_W), F32)
    else:
        k_ctx, v_ctx, h0 = ctx
        tables = axial_rope_tables(T)
        q_r = apply_axial_rope(q, tables)
        k_r = apply_axial_rope(k, tables)
        o_attn = latent_attention(q_r, k_r, v, k_ctx.astype(k.dtype), v_ctx.astype(v.dtype), sink)
    y_attn = o_attn.astype(x.dtype) * jax.nn.silu(z_attn)
    h_f, h_b = rglru_branch(u_lru, P['lru_conv_w'], P['lru_conv_b'], P['lru_gate_w'],
                            P['lru_gate_b'], P['lru_lambda'], h0)
    y_lru = (h_f + h_b).astype(x.dtype) * jax.nn.silu(z_lru)
    m = jax.nn.sigmoid(merge_logits).reshape(B, T, N_BRANCH, D_MODEL)
    merged = (m[:, :, 0] * (y_pool @ P['w_pool_o'])
              + m[:, :, 1] * (y_attn @ P['w_attn_o'])
              + m[:, :, 2] * (y_lru @ P['w_lru_o']))
    out = merged @ P['w_out']
    x_new = x + gate[:, None] * rms_norm(out, P['g_post'])
    if ctx is None:
        h_final = jnp.stack([h_f[:, -1], h_b[:, 0]], axis=1)
        return x_new, k, v, h_final
    return x_new


def setup_inputs(seed: int = 0) -> dict:
    key = jax.random.key(seed)
    ks = jax.random.split(key, 24)
    nrm = lambda i, shape, s: jax.random.normal(ks[i], shape, F32) * s
    a0 = jax.random.uniform(ks[19], (DEPTH, 2, LRU_W), F32, 0.9, 0.999)
    return {
        'x_prompt': nrm(0, (BATCH, SEQ, D_MODEL), 1.0),
        'x_sample': nrm(1, (DEC_BATCH, DEC_SEQ, D_MODEL), 1.0),
        'cache_k': nrm(2, (DEC_BATCH, DEPTH, PAST_LEN, N_KV, HEAD_DIM), 1.0),
        'cache_v': nrm(3, (DEC_BATCH, DEPTH, PAST_LEN, N_KV, HEAD_DIM), 1.0),
        'state_lru': nrm(4, (DEC_BATCH, DEPTH, 2, LRU_W), 0.5),
        'c': nrm(5, (DEC_BATCH, D_MODEL), 1.0),
        'c_ctx': nrm(6, (D_MODEL,), 1.0),
        'g_pre': 1.0 + nrm(7, (DEPTH, D_MODEL), 0.02),
        'g_post': 1.0 + nrm(8, (DEPTH, D_MODEL), 0.02),
        'w_ada': nrm(9, (DEPTH, D_MODEL, 3 * D_MODEL), 0.3 * D_MODEL ** -0.5),
        'b_ada': nrm(10, (DEPTH, 3 * D_MODEL), 0.01),
        'w_in': nrm(11, (DEPTH, D_MODEL, IN_W), D_MODEL ** -0.5),
        'w_pool_map': nrm(12, (DEPTH, POOL_GROUPS, POOL_GW, POOL_GW), POOL_GW ** -0.5),
        'pool_scale': 1.0 + nrm(13, (DEPTH, POOL_W), 0.1),
        'attn_sink': nrm(14, (DEPTH, N_HEADS), 0.5),
        'lru_conv_w': nrm(15, (DEPTH, CONV_W, LRU_W), CONV_W ** -0.5),
        'lru_conv_b': nrm(16, (DEPTH, LRU_W), 0.01),
        'lru_gate_w': nrm(17, (DEPTH, 2, 2, LRU_BLOCKS, LRU_BW, LRU_BW), LRU_BW ** -0.5),
        'lru_gate_b': nrm(18, (DEPTH, 2, 2, LRU_W), 0.01),
        'lru_lambda': jnp.log(a0) - jnp.log1p(-a0),
        'w_pool_o': nrm(20, (DEPTH, POOL_W, D_MODEL), POOL_W ** -0.5),
        'w_attn_o': nrm(21, (DEPTH, ATTN_W, D_MODEL), ATTN_W ** -0.5),
        'w_lru_o': nrm(22, (DEPTH, LRU_W, D_MODEL), LRU_W ** -0.5),
        'w_out': nrm(23, (DEPTH, D_MODEL, D_MODEL), D_MODEL ** -0.5),
    }


def reference(x_prompt, x_sample, cache_k, cache_v, state_lru, c, c_ctx, g_pre, g_post,
              w_ada, b_ada, w_in, w_pool_map, pool_scale, attn_sink, lru_conv_w, lru_conv_b,
              lru_gate_w, lru_gate_b, lru_lambda, w_pool_o, w_attn_o, w_lru_o, w_out):
    xp = x_prompt
    xs = x_sample
    new_k, new_v, new_h = [], [], []
    for l in range(DEPTH):
        P = {
            'g_pre': g_pre[l], 'g_post': g_post[l], 'w_in': w_in[l],
            'w_pool_map': w_pool_map[l], 'pool_scale': pool_scale[l], 'attn_sink': attn_sink[l],
            'lru_conv_w': lru_conv_w[l], 'lru_conv_b': lru_conv_b[l], 'lru_gate_w': lru_gate_w[l],
            'lru_gate_b': lru_gate_b[l], 'lru_lambda': lru_lambda[l],
            'w_pool_o': w_pool_o[l], 'w_attn_o': w_attn_o[l], 'w_lru_o': w_lru_o[l], 'w_out': w_out[l],
        }
        mod_ctx = (jax.nn.silu(c_ctx) @ w_ada[l] + b_ada[l])[None, :]
        mod_lat = jax.nn.silu(c) @ w_ada[l] + b_ada[l]
        xp, k_l, v_l, h_l = trunk_layer(xp, mod_ctx, P)
        new_k.append(k_l)
        new_v.append(v_l)
        new_h.append(h_l.astype(x_prompt.dtype))
        xs = trunk_layer(xs, mod_lat, P, ctx=(cache_k[:, l], cache_v[:, l], state_lru[:, l]))
    return (xp, xs, jnp.stack(new_k, axis=1), jnp.stack(new_v, axis=1), jnp.stack(new_h, axis=1))
```

```python
import numpy as np
import ml_dtypes
import concourse.bass as bass
import concourse.mybir as mybir
from concourse.bass_utils import run_bass_kernel_spmd

F32 = mybir.dt.float32
BF16 = mybir.dt.bfloat16
ALU = mybir.AluOpType
AF = mybir.ActivationFunctionType

D = 2048
KC = 16
INW = 10752
SEQ = 256
C_UP, C_ZP, C_Q, C_KV, C_ZA, C_UL, C_ZL, C_MG = 0, 512, 1024, 2048, 2560, 3584, 4096, 4608
S = 512
HALO = 128
EXT = S + 2 * HALO
EPS = 1e-6
POOL_W = (2, 4, 8, 16)
SAME_ENGINE_SYNC = True
DEBUG = False


class Buf:
    __slots__ = ("name", "w", "r", "dsem")

    def __init__(self, name):
        self.name = name
        self.w = None
        self.r = {}
        self.dsem = None


class DSem:
    def __init__(self, sem):
        self.sem = sem
        self.count = 0


class Plan:
    ENG = ("pe", "act", "dve", "pool", "sp")

    def __init__(self, nc):
        self.nc = nc
        self.prog = {e: [] for e in self.ENG}
        self.sem = {e: nc.alloc_semaphore("prog_" + e) for e in self.ENG}
        self.cnt = {e: 0 for e in self.ENG}
        self.seen = {e: {} for e in self.ENG}
        self.dsems = []
        self.nbuf = 0
        self.named = {}
        self.pending = {}

    def buf(self, name="b"):
        self.nbuf += 1
        return Buf(name + str(self.nbuf))

    def new_dsem(self):
        d = DSem(self.nc.alloc_semaphore("dma%d" % len(self.dsems)))
        self.dsems.append(d)
        return d

    def slot(self, name="s"):
        b = self.buf(name)
        b.dsem = name
        return b

    def slot_dsem(self, name, eng):
        key = (name, "sw" if eng == "pool" else "hw")
        if key not in self.named:
            self.named[key] = self.new_dsem()
        return self.named[key]

    def _waits(self, eng, reads, writes):
        need = {}
        own = self.sem[eng]

        def add(tok):
            if tok is None:
                return
            sem, val = tok
            if sem is own and (eng == "pe" or not SAME_ENGINE_SYNC):
                return
            k = id(sem)
            if self.seen[eng].get(k, 0) >= val:
                return
            if k not in need or need[k][1] < val:
                need[k] = (sem, val)

        for b in reads:
            add(b.w)
        for b in writes:
            add(b.w)
            for t in b.r.values():
                add(t)
        return list(need.values())

    def _commit(self, eng, waits, fn, tok, inc, reads, writes):
        self.prog[eng].append((waits, fn, tok[0], inc))
        for sem, val in waits:
            self.seen[eng][id(sem)] = val
        for b in reads:
            k = id(tok[0])
            if k not in b.r or b.r[k][1] < tok[1]:
                b.r[k] = tok
        for b in writes:
            b.w = tok
            b.r = {}

    def op(self, eng, fn, reads=(), writes=()):
        waits = self._waits(eng, reads, writes)
        self.cnt[eng] += 1
        tok = (self.sem[eng], self.cnt[eng])
        self._commit(eng, waits, fn, tok, 1, reads, writes)
        return tok

    def dma(self, eng, out, in_, reads, writes, dsem, **kw):
        if isinstance(dsem, str):
            dsem = self.slot_dsem(dsem, eng)
        waits = self._waits(eng, reads, writes)
        dsem.count += 16
        tok = (dsem.sem, dsem.count)
        self._commit(eng, waits, lambda e: e.dma_start(out=out, in_=in_, **kw), tok, 16, reads, writes)
        return tok

    def inherit(self, old_bufs, new_bufs):
        toks = self.pending
        for b in old_bufs:
            for t in ([b.w] if b.w else []) + list(b.r.values()):
                k = id(t[0])
                if k not in toks or toks[k][1] < t[1]:
                    toks[k] = t
        for nb in new_bufs:
            for k, t in toks.items():
                if k not in nb.r or nb.r[k][1] < t[1]:
                    nb.r[k] = t

    def final_wait(self, eng):
        waits = []
        for d in self.dsems:
            if d.count > 0:
                waits.append((d.sem, d.count))
        for e in self.ENG:
            if e != eng and self.cnt[e] > 0:
                waits.append((self.sem[e], self.cnt[e]))
        self.prog[eng].append((waits, None, None, 0))

    def replay(self, eng, e):
        for waits, fn, sem, inc in self.prog[eng]:
            for s, v in waits:
                e.wait_ge(s, v)
            if fn is not None:
                ins = fn(e)
                ins.then_inc(sem, inc)


class Arena:
    def __init__(self, nc, name, nbytes):
        self.t = nc.alloc_sbuf_tensor(name, [128, nbytes // 4], F32)
        self.nbytes = nbytes

    def ap(self, off, shape, dtype):
        esz = 4 if dtype == F32 else 2
        n = int(np.prod(shape))
        assert off % 4 == 0 and off + n * esz <= self.nbytes, (off, shape, self.nbytes)
        nw = (n * esz + 3) // 4
        a = self.t[:, off // 4: off // 4 + nw]
        if dtype != F32:
            a = a.bitcast(dtype)
        if len(shape) == 2:
            a = a.rearrange("p (a b) -> p a b", b=shape[1])
        elif len(shape) == 3:
            a = a.rearrange("p (a b c) -> p a b c", b=shape[1], c=shape[2])
        return a


def build_program(T, NP, DEPTH):
    assert T % S == 0 and NP % 2 == 0
    TOKS = T + NP * SEQ
    NG = TOKS // S
    NGS = T // S
    nc = bass.Bass("TRN2", target_bir_lowering=False)
    P = Plan(nc)

    def din(name, shape, dt=F32):
        return nc.dram_tensor(name, list(shape), dt, kind="ExternalInput").ap()

    def dout(name, shape, dt=F32):
        return nc.dram_tensor(name, list(shape), dt, kind="ExternalOutput").ap()

    def dscr(name, shape, dt=F32):
        return nc.dram_tensor(name, list(shape), dt, kind="Internal").ap()

    xs = din("xs", [T, D])
    xp = din("xp", [NP * SEQ, D])
    ck = din("ck", [DEPTH, 256, 256])
    cv = din("cv", [DEPTH, 256, 256])
    st_in = din("st", [DEPTH * 8, 128])
    c2 = din("c2", [32, 128])
    vp = din("vp", [DEPTH, 128, 128])
    sink = din("sink", [DEPTH, 8])
    w_ada = din("w_ada", [DEPTH, D, 3 * D])
    w_in = din("w_in", [DEPTH, D, INW])
    w_map = din("w_map", [DEPTH, 4, 128, 128])
    w_gate = din("w_gate", [DEPTH, 16, 128, 128])
    w_po = din("w_po", [DEPTH, 512, D])
    w_ao = din("w_ao", [DEPTH, 1024, D])
    w_lo = din("w_lo", [DEPTH, 512, D])
    w_out = din("w_out", [DEPTH, D, D])
    c_ident = din("c_ident", [128, 128])
    c_rot = din("c_rot", [128, 128])
    c_mask = din("c_mask", [2, 128, 512], BF16)
    c_cos = din("c_cos", [128, T + 2 * HALO])
    c_sin = din("c_sin", [128, T + 2 * HALO])
    c_edge = din("c_edge", [2, 32])

    ys = dout("ys", [T, D])
    yp = dout("yp", [NP * SEQ, D])
    nk = dout("nk", [NP, DEPTH, SEQ, 256])
    nv = dout("nv", [NP, DEPTH, SEQ, 256])
    nh = dout("nh", [NP, DEPTH, 2, 512])

    xa_s = dscr("xa_s", [T, D])
    xa_p = dscr("xa_p", [NP * SEQ, D])
    hT_d = dscr("hT_d", [KC, 128, TOKS], BF16)
    HS_d = dscr("HS_d", [4, 128, TOKS])
    XC_d = dscr("XC_d", [4, 128, TOKS])
    wb_in = dscr("wb_in", [DEPTH, D, INW], BF16)
    wb_map = dscr("wb_map", [DEPTH, 4, 128, 128], BF16)
    wb_gate = dscr("wb_gate", [DEPTH, 16, 128, 128], BF16)
    wb_po = dscr("wb_po", [DEPTH, 512, D], BF16)
    wb_ao = dscr("wb_ao", [DEPTH, 1024, D], BF16)
    wb_lo = dscr("wb_lo", [DEPTH, 512, D], BF16)
    wb_out = dscr("wb_out", [DEPTH, D, D], BF16)

    hT_b = [P.buf("hT") for _ in range(NG)]
    HS_b = [P.buf("HS") for _ in range(NG)]
    XC_b = [P.buf("XC") for _ in range(NG)]
    xa_b = [P.buf("xa") for _ in range(NG)]
    y_b = [P.buf("y") for _ in range(NG)]
    out_b = P.buf("outs")
    wconv = {}

    def x_src(l, g):
        first_out_is_y = (DEPTH - 1) % 2 == 0
        if l == 0:
            src, b = None, None
        else:
            prev_is_y = ((DEPTH - 1 - (l - 1)) % 2 == 0)
            src, b = ("y", y_b[g]) if prev_is_y else ("xa", xa_b[g])
        if g < NGS:
            t0 = g * S
            tens = xs if l == 0 else (ys if src == "y" else xa_s)
        else:
            t0 = (g - NGS) * S
            tens = xp if l == 0 else (yp if src == "y" else xa_p)
        return tens[t0:t0 + S, :], b

    def x_dst(l, g):
        is_y = ((DEPTH - 1 - l) % 2 == 0)
        if g < NGS:
            t0 = g * S
            tens = ys if is_y else xa_s
        else:
            t0 = (g - NGS) * S
            tens = yp if is_y else xa_p
        return tens[t0:t0 + S, :], (y_b[g] if is_y else xa_b[g])

    WSL = 4
    wring = [nc.alloc_sbuf_tensor("wr%d" % i, [128, KC, 512], BF16) for i in range(WSL)]
    wring_b = [P.slot("wr%d" % i) for i in range(WSL)]
    wr_i = [0]

    def wslot():
        i = wr_i[0] % WSL
        wr_i[0] += 1
        return wring[i], wring_b[i]

    ident = nc.alloc_sbuf_tensor("ident", [128, 128], F32)
    rot = nc.alloc_sbuf_tensor("rot", [128, 128], F32)
    ident_b = nc.alloc_sbuf_tensor("ident_b", [128, 128], BF16)
    masks = nc.alloc_sbuf_tensor("masks", [128, 2, 512], BF16)
    ones_b = nc.alloc_sbuf_tensor("ones_b", [128, 128], BF16)
    ones_f = nc.alloc_sbuf_tensor("ones_f", [128, 128], F32)
    edge = nc.alloc_sbuf_tensor("edge", [128, 2, 32], F32)
    VPT = nc.alloc_sbuf_tensor("VPT", [128, DEPTH, 128], F32)
    H0T = nc.alloc_sbuf_tensor("H0T", [128, DEPTH * 8], F32)
    cTb = nc.alloc_sbuf_tensor("cTb", [128, 32], BF16)
    cTf = nc.alloc_sbuf_tensor("cTf", [128, 32], F32)
    modT = nc.alloc_sbuf_tensor("modT", [128, DEPTH, 48, 2], F32)
    Amod = nc.alloc_sbuf_tensor("Amod", [128, DEPTH, 2, 16], F32)
    Gmod = nc.alloc_sbuf_tensor("Gmod", [128, DEPTH, 2, 16], F32)
    spT = nc.alloc_sbuf_tensor("spT", [128, DEPTH, 8], F32)
    es8 = nc.alloc_sbuf_tensor("es8", [128, DEPTH, 8], F32)
    wmap_s = nc.alloc_sbuf_tensor("wmap_s", [128, 4, 128], BF16)
    wgate_s = nc.alloc_sbuf_tensor("wgate_s", [128, 16, 128], BF16)
    KTc = nc.alloc_sbuf_tensor("KTc", [128, 2, 256], BF16)
    Vc = nc.alloc_sbuf_tensor("Vc", [128, 2, 256], BF16)
    small = nc.alloc_sbuf_tensor("small", [128, 64], F32)
    const_b = P.buf("const")
    vpt_b = P.buf("vpt")
    lay_b = P.slot("layw")
    ctx_b = P.buf("ctx")
    small_b = P.buf("small")

    ARENA_BYTES = 110 * 1024
    AR = Arena(nc, "arena", ARENA_BYTES)
    arena_live = []

    def abuf(name, lo, nbytes, slot=False):
        hi = lo + nbytes
        nb = P.slot(name) if slot else P.buf(name)
        keep = []
        for (a, b, ob) in arena_live:
            if a < hi and lo < b:
                for t in ([ob.w] if ob.w else []) + list(ob.r.values()):
                    k = id(t[0])
                    if k not in nb.r or nb.r[k][1] < t[1]:
                        nb.r[k] = t
                if lo <= a and b <= hi:
                    continue
            keep.append((a, b, ob))
        arena_live[:] = keep + [(lo, hi, nb)]
        return nb

    def carve(name, off, shape, dtype, slot=False):
        esz = 4 if dtype == F32 else 2
        nbytes = (int(np.prod(shape)) * esz + 3) // 4 * 4
        return AR.ap(off, shape, dtype), abuf(name, off, nbytes, slot)

    psum = [nc.alloc_psum_tensor("ps%d" % i, [128, 512], F32) for i in range(8)]
    psum_b = [P.buf("ps") for _ in range(8)]
    ps_i = [0]

    def bank():
        i = ps_i[0] % 8
        ps_i[0] += 1
        return psum[i], psum_b[i]

    DBG = {}
    dbg_slot = P.slot("dbg")

    def dbg(name, ap, rbuf, n):
        if not DEBUG or name in DBG:
            return
        t = nc.dram_tensor("dbg_" + name, [128, n], F32, kind="ExternalOutput").ap()
        DBG[name] = t
        P.dma("sp", t, ap, [rbuf], [out_b], dbg_slot.dsem)

    def mm(out, pairs, reads, wbuf, start=True, stop=True):
        def fn(e):
            ins = None
            n = len(pairs)
            for i, (l, r) in enumerate(pairs):
                ins = e.matmul(out, lhsT=l, rhs=r, start=(start and i == 0), stop=(stop and i == n - 1))
            return ins
        return P.op("pe", fn, reads=reads, writes=[wbuf])

    def tr(out, in_, idn, reads, wbuf):
        return P.op("pe", lambda e: e.transpose(out=out, in_=in_, identity=idn), reads=reads, writes=[wbuf])

    def act(out, in_, func, reads, writes, **kw):
        return P.op("act", lambda e: e.activation(out=out, in_=in_, func=func, **kw), reads=reads, writes=writes)

    def dve(fn, reads, writes):
        return P.op("dve", fn, reads=reads, writes=writes)

    def load(out, in_, reads, wbuf, eng="sp", **kw):
        return P.dma(eng, out, in_, reads, [wbuf], wbuf.dsem, **kw)

    def store(out, in_, rbuf, wbufs, eng="pool", extra_reads=()):
        return P.dma(eng, out, in_, [rbuf] + list(extra_reads), wbufs, rbuf.dsem)

    def wpiece(dram_ap, reads=(), eng="sp", nk=KC, ncols=512):
        t, b = wslot()
        load(t[:, 0:nk, 0:ncols], dram_ap, list(reads), b, eng=eng)
        return t, b

    def w_in_piece(l, c0, ncols=512):
        return wpiece(wb_in[l, :, c0:c0 + ncols].rearrange("(kc p) c -> p kc c", p=128),
                      reads=[wconv[("in", l, in_group(c0))]], ncols=ncols)

    pro = {}
    vpin, pro["vpin"] = carve("vpin", 0, [128], F32, True)
    tmpa, pro["tmpa"] = carve("tmpa", 1024, [128], F32, True)
    tmpb, pro["tmpb"] = carve("tmpb", 2048, [128], F32)
    cslot = P.slot("cload")
    load(ident[:], c_ident, [], cslot)
    load(rot[:], c_rot, [], cslot)
    load(masks[:], c_mask.rearrange("m p c -> p m c"), [], cslot)
    load(edge[:].rearrange("p a b -> p (a b)"), c_edge.rearrange("a b -> (a b)").partition_broadcast(128), [], cslot)
    const_b.w = cslot.w
    dve(lambda e: e.tensor_copy(out=ident_b[:], in_=ident[:]), [const_b], [const_b])
    dve(lambda e: e.memset(ones_b[:], 1.0), [], [const_b])
    dve(lambda e: e.memset(ones_f[:], 1.0), [], [const_b])
    dve(lambda e: e.memset(small[:, 0:1], EPS), [], [const_b])
    dve(lambda e: e.memset(small[:, 1:2], 0.0), [], [const_b])
    dve(lambda e: e.memset(small[:, 2:3], 1.0), [], [const_b])

    def in_group(c0):
        return "A" if c0 == C_UL else ("C" if c0 >= C_MG else "B")

    def conv_gen(l):
        full = lambda key, dst, src, nrows, rows_per: (key, dst, src, nrows, rows_per)
        in_cols = [C_UL, C_UP, C_ZP, C_KV, C_ZA, C_Q, C_ZA + 512, C_Q + 512, C_ZL]
        mg_cols = [C_MG + b * D + jj * 512 for jj in range(4) for b in range(3)]
        inp = lambda c0: full(("in", l, in_group(c0)), wb_in[l][:, c0:c0 + 512], w_in[l][:, c0:c0 + 512], D, 512)
        specs = [full(("gate", l), wb_gate[l].rearrange("g c e -> (g c) e"), w_gate[l].rearrange("g c e -> (g c) e"), 2048, 1024),
                 full(("map", l), wb_map[l].rearrange("g c e -> (g c) e"), w_map[l].rearrange("g c e -> (g c) e"), 512, 512)]
        specs += [inp(c0) for c0 in in_cols]
        specs += [full(("po", l), wb_po[l], w_po[l], 512, 128), full(("ao", l), wb_ao[l], w_ao[l], 1024, 128),
                  full(("lo", l), wb_lo[l], w_lo[l], 512, 128)]
        specs += [inp(c0) for c0 in mg_cols]
        specs += [full(("out", l), wb_out[l], w_out[l], D, 128)]
        for key, dst, src, nrows, rows_per in specs:
            if key not in wconv:
                wconv[key] = P.buf("wc")
        for key, dst, src, nrows, rows_per in specs:
            ck_ = ("conv", key[0], key[2] if len(key) > 2 else -1, l % 2)
            if ck_ not in P.named:
                P.named[ck_] = P.new_dsem()
            ds = P.named[ck_]
            for r0 in range(0, nrows, rows_per):
                P.dma("pool", dst[r0:r0 + rows_per], src[r0:r0 + rows_per], [], [wconv[key]], ds)
                yield

    def drain(it, n=None):
        if it is None:
            return
        k = 0
        for _ in it:
            k += 1
            if n is not None and k >= n:
                return

    for l in range(DEPTH):
        load(vpin, vp[l], [], pro["vpin"])
        pt, pb = bank()
        tr(pt[:, 0:128], vpin, ident[:], [pro["vpin"], const_b], pb)
        act(VPT[:, l, :], pt[:, 0:128], AF.Copy, [pb], [vpt_b])
    load(vpin[0:DEPTH * 8, :], st_in, [], pro["vpin"])
    pt, pb = bank()
    tr(pt[:, 0:DEPTH * 8], vpin[0:DEPTH * 8, :], ident[0:DEPTH * 8, 0:DEPTH * 8], [pro["vpin"], const_b], pb)
    act(H0T[:], pt[:, 0:DEPTH * 8], AF.Copy, [pb], [vpt_b])
    load(vpin[0:32, :], c2, [], pro["vpin"])
    pt, pb = bank()
    tr(pt[:, 0:32], vpin[0:32, :], ident[0:32, 0:32], [pro["vpin"], const_b], pb)
    act(cTf[:], pt[:, 0:32], AF.Silu, [pb], [vpt_b])
    for l in range(DEPTH):
        act(tmpa[:, 0:8], VPT[:, l, 120:128], AF.Exp, [vpt_b], [pro["tmpa"]], scale=-1.0)
        act(tmpb[:, 0:8], tmpa[:, 0:8], AF.Ln, [pro["tmpa"], const_b], [pro["tmpb"]], bias=small[:, 2:3])
        if l == 0:
            dbg("lam", VPT[:, 0, 120:128], vpt_b, 8)
            dbg("tmpa", tmpa[:, 0:8], pro["tmpa"], 8)
            dbg("tmpb", tmpb[:, 0:8], pro["tmpb"], 8)
        dve(lambda e, l=l: e.tensor_scalar_mul(out=spT[:, l, :], in0=tmpb[:, 0:8], scalar1=-8.0), [pro["tmpb"]], [vpt_b])
        load(tmpa[:, 8:16], sink[l].partition_broadcast(128), [], pro["tmpa"], eng="sp")
        act(es8[:, l, :], tmpa[:, 8:16], AF.Exp, [pro["tmpa"]], [vpt_b])
        if l == 0:
            dbg("spT", spT[:, 0, :], vpt_b, 8)
            dbg("es8", es8[:, 0, :], vpt_b, 8)

    NADA = 24

    def ada_piece(l, pc):
        mb_ = mod_bs[l]
        t, b = wslot()
        t32 = t[:].rearrange("p a b -> p (a b)").bitcast(F32).rearrange("p (a b) -> p a b", b=256)
        load(t32, w_ada[l, :, pc * 256:(pc + 1) * 256].rearrange("(kc p) c -> p kc c", p=128), [], b, eng="sp")
        pt, pb = bank()
        for oc in range(2):
            mm(pt[:, oc * 2:oc * 2 + 2],
               [(t32[:, kc, oc * 128:(oc + 1) * 128], cTf[:, kc * 2:kc * 2 + 2]) for kc in range(KC)], [b, vpt_b], pb)
        for w in range(2):
            dve(lambda e, l=l, w=w, pt=pt, pc=pc: e.tensor_tensor(
                out=modT[:, l, pc * 2:(pc + 1) * 2, w], in0=pt[:, 0:4].rearrange("p (a b) -> p a b", b=2)[:, :, w],
                in1=VPT[:, l, 32 + pc * 2:34 + pc * 2], op=ALU.add), [pb, vpt_b], [mb_])
        if pc == NADA - 1:
            for w in range(2):
                dve(lambda e, l=l, w=w: e.scalar_tensor_tensor(out=Amod[:, l, w, :], in0=modT[:, l, 16:32, w], scalar=1.0,
                                                               in1=VPT[:, l, 0:16], op0=ALU.add, op1=ALU.mult),
                    [mb_, vpt_b], [mb_])
                dve(lambda e, l=l, w=w: e.tensor_tensor(out=Gmod[:, l, w, :], in0=modT[:, l, 32:48, w],
                                                        in1=VPT[:, l, 16:32], op=ALU.mult), [mb_, vpt_b], [mb_])

    def ada_gen(l):
        for pc in range(NADA):
            ada_piece(l, pc)
            yield

    def ada_piece_load(l, pc, slot_fn):
        raise NotImplementedError

    def ada_piece_compute(l, pc, t, b):
        raise NotImplementedError

    mod_bs = [P.buf("mod") for _ in range(DEPTH)]
    drain(ada_gen(0))
    drain(conv_gen(0))
    bg = {"it": None, "ada": None}

    def layer(l):
        load(wmap_s[:], wb_map[l].rearrange("g c e -> c g e"), [wconv[("map", l)]], lay_b)
        load(wgate_s[:], wb_gate[l].rearrange("g c e -> c g e"), [wconv[("gate", l)]], lay_b)
        cx = {}
        ckin, cx["ckin"] = carve("ckin", 0, [2, 256], F32, True)
        cvin, cx["cvin"] = carve("cvin", 2048, [2, 256], F32, True)
        load(ckin, ck[l].rearrange("(b p) c -> p b c", p=128), [], cx["ckin"])
        load(cvin, cv[l].rearrange("(b p) c -> p b c", p=128), [], cx["cvin"])
        pt, pb = bank()
        for kv in range(2):
            for cb in range(2):
                tr(pt[:, (kv * 2 + cb) * 128:(kv * 2 + cb + 1) * 128], ckin[:, cb, kv * 128:(kv + 1) * 128], ident[:],
                   [cx["ckin"], const_b], pb)
        act(KTc[:].rearrange("p a b -> p (a b)"), pt[:, 0:512], AF.Copy, [pb], [ctx_b])
        dve(lambda e: e.tensor_copy(out=Vc[:], in_=cvin), [cx["cvin"]], [ctx_b])

        def chain(*its):
            for it in its:
                for _ in it:
                    yield
        bg["ada"] = ada_gen(l + 1) if l + 1 < DEPTH else None
        ada_l = None
        ada_state["l"] = ada_l
        ada_state["next"] = 0
        ada_state["pending"] = None
        bg["it"] = conv_gen(l + 1) if l + 1 < DEPTH else None
        if l == 0:
            stage_N(l)
        lru_stage(l, 0, T, S, None)
        for i in range(NP):
            lru_stage(l, T + i * SEQ, SEQ, SEQ, i)
        drain(bg["ada"])
        for g in range(NG):
            main_tile(l, g)
        drain(bg["it"])
        ada_flush()
        for f in deferred:
            f()
        deferred.clear()

    def stage_N(l):
        sb = {}
        xin = [None, None]
        htt = [None, None]
        xin[0], sb["xin0"] = carve("xin0", 0, [D], F32, True)
        xin[1], sb["xin1"] = carve("xin1", 8192, [D], F32, True)
        junk, sb["junk"] = carve("junk", 16384, [D], BF16)
        htt[0], sb["ht0"] = carve("ht0", 20480, [KC, 512], BF16, True)
        htt[1], sb["ht1"] = carve("ht1", 20480 + 16384, [KC, 512], BF16, True)
        ssq, sb["ss"] = carve("ss", 20480 + 32768, [16], F32)
        it = 0
        for g in range(NG):
            w = 0 if g < NGS else 1
            src, sbuf_d = x_src(l, g)
            ht, htb = htt[g % 2], sb["ht%d" % (g % 2)]
            for j in range(4):
                xi, xb = xin[it % 2], sb["xin%d" % (it % 2)]
                col = it % 8
                it += 1
                load(xi, src[j * 128:(j + 1) * 128, :], [sbuf_d] if sbuf_d else [], xb)
                act(junk, xi, AF.Square, [xb], [sb["junk"], sb["ss"]], accum_out=ssq[:, col:col + 1],
                    scale=float(D ** -0.5))
                act(ssq[:, 8 + col:9 + col], ssq[:, col:col + 1], AF.Sqrt, [sb["ss"], const_b], [sb["ss"]],
                    bias=small[:, 0:1])
                dve(lambda e, col=col: e.reciprocal(out=ssq[:, 8 + col:9 + col], in_=ssq[:, 8 + col:9 + col]),
                    [sb["ss"]], [sb["ss"]])
                dve(lambda e, xi=xi, col=col: e.tensor_scalar_mul(out=xi, in0=xi, scalar1=ssq[:, 8 + col:9 + col]),
                    [xb, sb["ss"]], [xb])
                for q in range(4):
                    pt, pb = bank()
                    for c4 in range(4):
                        c = q * 4 + c4
                        tr(pt[:, c4 * 128:(c4 + 1) * 128], xi[:, c * 128:(c + 1) * 128], ident[:], [xb, const_b], pb)
                    for c4 in range(4):
                        c = q * 4 + c4
                        o = ht[:, c, j * 128:(j + 1) * 128]
                        i_ = pt[:, c4 * 128:(c4 + 1) * 128]
                        if c4 % 2 == 0:
                            act(o, i_, AF.Identity, [pb, mod_bs[l]], [htb], scale=Amod[:, l, w, c:c + 1],
                                bias=modT[:, l, c, w:w + 1])
                        else:
                            dve(lambda e, o=o, i_=i_, c=c, w=w: e.tensor_scalar(
                                out=o, in0=i_, scalar1=Amod[:, l, w, c:c + 1], scalar2=modT[:, l, c, w:w + 1],
                                op0=ALU.mult, op1=ALU.add), [pb, mod_bs[l]], [htb])
            store(hT_d[:, :, g * S:(g + 1) * S].rearrange("c p t -> p c t"), ht, htb, [hT_b[g]], eng="sp")

    def lru_stage(l, tb, TS, TL, pidx):
        nt = TS // TL
        W3 = TL + 3
        sb = {}
        o = 0 if pidx is None else (pidx % 2) * 56320
        hx, sb["hx"] = carve("hx_%d_%d" % (TL, o), o, [KC, W3 + 1], BF16, True); o_hx = o; o += KC * (W3 + 1) * 2
        U, sb["U"] = carve("U", o, [4, W3 + 1], F32); o += 4 * (W3 + 1) * 4
        XC = []
        for i in range(3):
            a_, sb["XC%d" % i] = carve("XC%d_%d_%d" % (i, TL, o), o, [4, TL], F32, True)
            XC.append(a_); o += 4 * TL * 4
        XCb, sb["XCb"] = carve("XCb", o, [4, TL], BF16); o += 4 * TL * 2
        HF = []
        for i in range(3):
            a_, sb["HF%d" % i] = carve("HF%d_%d_%d" % (i, TL, o), o, [4, TL], F32, True)
            HF.append(a_); o += 4 * TL * 4
        Gr, Gi, Ga, Gb = [], [], [], {}
        for n in range(4):
            a_, Gb[("r", n)] = carve("Gr%d" % n, o, [TL], F32); Gr.append(a_); o += TL * 4
            a_, Gb[("i", n)] = carve("Gi%d" % n, o, [TL], F32); Gi.append(a_); o += TL * 4
            a_, Gb[("a", n)] = carve("Ga%d" % n, o, [TL], F32); Ga.append(a_); o += TL * 4
        assert o <= ARENA_BYTES, o
        cw = lambda j, n: VPT[:, l, 84 + j * 4 + n: 85 + j * 4 + n]
        cb = lambda n: VPT[:, l, 100 + n:101 + n]
        gb = lambda d, k, n: VPT[:, l, 104 + (d * 2 + k) * 4 + n: 105 + (d * 2 + k) * 4 + n]

        def grp(t0):
            return (tb + t0) // S

        def gates_scan_all(d, xc, xcb_, outs, outb, inits, init_reads, rev):
            banks = []
            for n in range(4):
                p1, b1 = bank()
                mm(p1[:, 0:TL], [(wgate_s[:, (d * 2 + 0) * 4 + n, :], XCb[:, n, :])], [lay_b, sb["XCb"]], b1)
                p2, b2 = bank()
                mm(p2[:, 0:TL], [(wgate_s[:, (d * 2 + 1) * 4 + n, :], XCb[:, n, :])], [lay_b, sb["XCb"]], b2)
                banks.append((p1, b1, p2, b2))
            for n in range(4):
                p1, b1, p2, b2 = banks[n]
                act(Gr[n], p1[:, 0:TL], AF.Sigmoid, [b1, vpt_b], [Gb[("r", n)]], bias=gb(d, 0, n))
                act(Gi[n], p2[:, 0:TL], AF.Sigmoid, [b2, vpt_b], [Gb[("i", n)]], bias=gb(d, 1, n))
            for n in range(4):
                act(Ga[n], Gr[n], AF.Exp, [Gb[("r", n)], vpt_b], [Gb[("a", n)]], scale=spT[:, l, d * 4 + n: d * 4 + n + 1])
            for n in range(4):
                dve(lambda e, n=n: e.tensor_tensor(out=Gr[n], in0=Ga[n], in1=Ga[n], op=ALU.mult),
                    [Gb[("a", n)]], [Gb[("r", n)]])
                dve(lambda e, n=n: e.tensor_tensor(out=Gi[n], in0=Gi[n], in1=xc[:, n, :], op=ALU.mult),
                    [Gb[("i", n)], xcb_], [Gb[("i", n)]])
            for n in range(4):
                act(Gr[n], Gr[n], AF.Sqrt, [Gb[("r", n)], const_b], [Gb[("r", n)]], bias=small[:, 2:3], scale=-1.0)
            for n in range(4):
                dve(lambda e, n=n: e.tensor_tensor(out=Gi[n], in0=Gi[n], in1=Gr[n], op=ALU.mult),
                    [Gb[("i", n)], Gb[("r", n)]], [Gb[("i", n)]])
                o_ = outs[:, n, :]
                if rev:
                    dve(lambda e, n=n, o_=o_: e.tensor_tensor_scan(out=o_[:, ::-1], data0=Ga[n][:, ::-1],
                                                                   data1=Gi[n][:, ::-1], initial=inits[n],
                                                                   op0=ALU.mult, op1=ALU.add),
                        [Gb[("a", n)], Gb[("i", n)]] + init_reads, [outb])
                else:
                    dve(lambda e, n=n, o_=o_: e.tensor_tensor_scan(out=o_, data0=Ga[n], data1=Gi[n], initial=inits[n],
                                                                   op0=ALU.mult, op1=ALU.add),
                        [Gb[("a", n)], Gb[("i", n)]] + init_reads, [outb])

        single = (nt == 1)

        def phaseA(i):
            t0 = i * TL
            h, hb = hx, sb["hx"]
            lo = max(t0 - 2, 0)
            hi = min(t0 + TL + 1, TS)
            if t0 == 0:
                dve(lambda e, h=h: e.memset(h[:, :, 0:2], 0.0), [], [hb])
            if t0 + TL + 1 > TS:
                dve(lambda e, h=h: e.memset(h[:, :, TL + 2:TL + 3], 0.0), [], [hb])
            gs = sorted(set([grp(lo), grp(hi - 1)]))
            load(h[:, :, lo - (t0 - 2): hi - (t0 - 2)], hT_d[:, :, tb + lo: tb + hi].rearrange("c p t -> p c t"),
                 [hT_b[g] for g in gs], hb)
            xc, xcb_ = XC[i % 3], sb["XC%d" % (i % 3)]
            drain(bg["ada"], 2)
            wt, wbf = w_in_piece(l, C_UL)
            for n in range(4):
                p1, b1 = bank()
                mm(p1[:, 0:TL], [(wt[:, kc, n * 128:(n + 1) * 128], h[:, kc, 0:TL]) for kc in range(KC)], [wbf, hb], b1)
                p3, b3 = bank()
                mm(p3[:, 0:3], [(wt[:, kc, n * 128:(n + 1) * 128], h[:, kc, TL:TL + 3]) for kc in range(KC)],
                   [wbf, hb], b3)
                act(U[:, n, 0:TL], p1[:, 0:TL], AF.Copy, [b1], [sb["U"]])
                act(U[:, n, TL:TL + 3], p3[:, 0:3], AF.Copy, [b3], [sb["U"]])
                dve(lambda e, n=n, xc=xc: e.tensor_scalar(out=xc[:, n, :], in0=U[:, n, 0:TL], scalar1=cw(0, n),
                                                          scalar2=cb(n), op0=ALU.mult, op1=ALU.add),
                    [sb["U"], vpt_b], [xcb_])
                for j in range(1, 4):
                    dve(lambda e, n=n, xc=xc, j=j: e.scalar_tensor_tensor(out=xc[:, n, :], in0=U[:, n, j:j + TL],
                                                                          scalar=cw(j, n), in1=xc[:, n, :],
                                                                          op0=ALU.mult, op1=ALU.add),
                        [sb["U"], vpt_b], [xcb_])

        def phaseB(i):
            t0 = i * TL
            xc, xcb_ = XC[i % 3], sb["XC%d" % (i % 3)]
            act(XCb, xc, AF.Copy, [xcb_], [sb["XCb"]])
            if not single:
                store(XC_d[:, :, tb + t0: tb + t0 + TL].rearrange("n p t -> p n t"), xc, xcb_, [XC_b[grp(t0)]])
            hf, hfb = HF[i % 3], sb["HF%d" % (i % 3)]
            if i == 0:
                inits = [0.0 if pidx is not None else H0T[:, (l * 2 + 0) * 4 + n:(l * 2 + 0) * 4 + n + 1] for n in range(4)]
                ir = [vpt_b]
            else:
                inits = [HF[(i - 1) % 3][:, n, TL - 1:TL] for n in range(4)]
                ir = [sb["HF%d" % ((i - 1) % 3)]]
            gates_scan_all(0, xc, xcb_, hf, hfb, inits, ir, False)
            if not single:
                store(HS_d[:, :, tb + t0: tb + t0 + TL].rearrange("n p t -> p n t"), hf, hfb, [HS_b[grp(t0)]])
            if pidx is not None and i == nt - 1:
                for n in range(4):
                    P.dma("pool", nh[pidx, l, 0, n * 128:(n + 1) * 128].rearrange("(p o) -> p o", o=1),
                          hf[:, n, TL - 1:TL], [hfb], [out_b], hfb.dsem)

        phaseA(0)
        for i in range(nt):
            if i + 1 < nt:
                phaseA(i + 1)
            phaseB(i)
        HB = []
        for i in range(2):
            a_, sb["HB%d" % i] = carve("HB%d_%d_%d" % (i, TL, o_hx), o_hx + i * 4 * TL * 4, [4, TL], F32, True)
            HB.append(a_)
        for k in range(nt):
            i = nt - 1 - k
            t0 = i * TL
            if single:
                xc, xcb_ = XC[0], sb["XC0"]
                hf, hfb = HF[0], sb["HF0"]
            else:
                xc, xcb_ = XC[k % 3], sb["XC%d" % (k % 3)]
                hf, hfb = HF[k % 3], sb["HF%d" % (k % 3)]
                load(xc, XC_d[:, :, tb + t0: tb + t0 + TL].rearrange("n p t -> p n t"), [XC_b[grp(t0)]], xcb_)
                load(hf, HS_d[:, :, tb + t0: tb + t0 + TL].rearrange("n p t -> p n t"), [HS_b[grp(t0)]], hfb)
                act(XCb, xc, AF.Copy, [xcb_], [sb["XCb"]])
            hbk, hbb = HB[k % 2], sb["HB%d" % (k % 2)]
            if k == 0:
                inits = [0.0 if pidx is not None else H0T[:, (l * 2 + 1) * 4 + n:(l * 2 + 1) * 4 + n + 1] for n in range(4)]
                ir = [vpt_b]
            else:
                inits = [HB[(k - 1) % 2][:, n, 0:1] for n in range(4)]
                ir = [sb["HB%d" % ((k - 1) % 2)]]
            gates_scan_all(1, xc, xcb_, hbk, hbb, inits, ir, True)
            dve(lambda e, hf=hf, hbk=hbk: e.tensor_tensor(out=hf, in0=hf, in1=hbk, op=ALU.add), [hfb, hbb], [hfb])
            store(HS_d[:, :, tb + t0: tb + t0 + TL].rearrange("n p t -> p n t"), hf, hfb, [HS_b[grp(t0)]])
            if pidx is not None and i == 0:
                for n in range(4):
                    P.dma("pool", nh[pidx, l, 1, n * 128:(n + 1) * 128].rearrange("(p o) -> p o", o=1),
                          hbk[:, n, 0:1], [hbb], [out_b], hbb.dsem)

    O_HT, O_Y, O_MG, O_T = 0, 24576, 40960, 57344
    O_GG = O_T + 26624
    pre = {}
    deferred = []
    ada_state = {"l": None, "next": 0, "pending": None}

    def ada_compute_pending():
        if ada_state["pending"] is not None:
            pc, t, b = ada_state["pending"]
            ada_piece_compute(ada_state["l"], pc, t, b)
            ada_state["pending"] = None

    def ada_issue_load():
        if ada_state["l"] is None or ada_state["next"] >= 12:
            return
        pc = ada_state["next"]
        ada_state["next"] += 1
        t, b = ada_piece_load(ada_state["l"], pc, lambda: carve("adaw", O_MG, [KC, 512], BF16, True))
        ada_state["pending"] = (pc, t, b)

    def ada_flush():
        ada_compute_pending()
        while ada_state["l"] is not None and ada_state["next"] < 12:
            pc = ada_state["next"]
            ada_state["next"] += 1
            t, b = ada_piece_load(ada_state["l"], pc, wslot)
            ada_piece_compute(ada_state["l"], pc, t, b)

    def load_hT(l, g):
        sample = g < NGS
        hT, hb = carve("hT", O_HT, [KC, EXT], BF16, True)
        if sample:
            t0 = g * S
            lo, hi = max(t0 - HALO, 0), min(t0 + S + HALO, T)
            if t0 == 0:
                dve(lambda e: e.memset(hT[:, :, 0:HALO], 0.0), [], [hb])
            if t0 + S + HALO > T:
                dve(lambda e: e.memset(hT[:, :, HALO + S:EXT], 0.0), [], [hb])
            gs = sorted(set([lo // S, (hi - 1) // S]) | {g})
            load(hT[:, :, lo - (t0 - HALO): hi - (t0 - HALO)], hT_d[:, :, lo:hi].rearrange("c p t -> p c t"),
                 [hT_b[x] for x in gs], hb)
        else:
            load(hT[:, :, 0:S], hT_d[:, :, g * S:g * S + S].rearrange("c p t -> p c t"), [hT_b[g]], hb)
        return hT, hb

    def main_tile(l, g):
        sample = g < NGS
        w = 0 if sample else 1
        t0 = g * S if sample else (g - NGS) * S
        tb = g * S
        mod_b = mod_bs[l]
        if (l, g) in pre:
            hT, hb = pre.pop((l, g))
        else:
            hT, hb = load_hT(l, g)
        if sample:
            so = HALO
            ext_tiles = [(0, 512), (512, 768)]
            segs = [(0, S, t0 == 0, t0 + S == T)]
        else:
            so = 0
            ext_tiles = [(0, 512)]
            segs = [(0, SEQ, True, True), (SEQ, SEQ, True, True)]
        hS = lambda kc: hT[:, kc, so:so + S]

        pb_ = {}
        o = O_T
        Uext, pb_["Uext"] = carve("Uext", o, [4, EXT], F32); o += 4 * EXT * 4
        Upad, pb_["Upad"] = carve("Upad", o, [S + 16], F32); o += (S + 16) * 4
        TA, pb_["TA"] = carve("TA", o, [S + 16], F32); o += (S + 16) * 4
        TBf, pb_["TB"] = carve("TB", o, [S + 16], F32); o += (S + 16) * 4
        dB, pb_["dB"] = carve("dB", o, [4, S], BF16); o += 4 * S * 2
        t8, pb_["t8"] = carve("t8", o, [16], F32); o += 64
        assert o <= O_T + 24576
        wt, wbf = w_in_piece(l, C_UP)
        for (a, b) in ext_tiles:
            for gi in range(4):
                p1, b1 = bank()
                mm(p1[:, 0:b - a], [(wt[:, kc, gi * 128:(gi + 1) * 128], hT[:, kc, a:b]) for kc in range(KC)], [wbf, hb], b1)
                act(Uext[:, gi, a:b], p1[:, 0:b - a], AF.Copy, [b1], [pb_["Uext"]])
        for f in deferred:
            f()
        deferred.clear()
        for gi in range(4):
            wv = POOL_W[gi]
            for (s0, L, first, last) in segs:
                UP, TAb, TBb = [pb_["Upad"]], [pb_["TA"]], [pb_["TB"]]
                if sample:
                    dve(lambda e, gi=gi: e.tensor_copy(out=Upad[:, 0:S + 16], in_=Uext[:, gi, HALO - 8:HALO + S + 8]),
                        [pb_["Uext"]], UP)
                else:
                    dve(lambda e, L=L: e.memset(Upad[:, 0:8], 0.0), [], UP)
                    dve(lambda e, L=L: e.memset(Upad[:, 8 + L:16 + L], 0.0), [], UP)
                    dve(lambda e, gi=gi, L=L, s0=s0: e.tensor_copy(out=Upad[:, 8:8 + L], in_=Uext[:, gi, s0:s0 + L]),
                        [pb_["Uext"]], UP)
                dve(lambda e, L=L: e.tensor_tensor(out=TA[:, 1:L + 16], in0=Upad[:, 0:L + 15], in1=Upad[:, 1:L + 16],
                                                   op=ALU.add), UP, TAb)
                Sw, Swb = TA, TAb
                if wv >= 4:
                    dve(lambda e, L=L: e.tensor_tensor(out=TBf[:, 2:L + 14], in0=TA[:, 1:L + 13], in1=TA[:, 3:L + 15],
                                                       op=ALU.add), TAb, TBb)
                    Sw, Swb = TBf, TBb
                if wv >= 8:
                    dve(lambda e, L=L: e.tensor_tensor(out=TA[:, 4:L + 12], in0=TBf[:, 2:L + 10], in1=TBf[:, 6:L + 14],
                                                       op=ALU.add), TBb, TAb)
                    Sw, Swb = TA, TAb
                if wv >= 16:
                    dve(lambda e, L=L: e.tensor_tensor(out=TBf[:, 8:L + 8], in0=TA[:, 4:L + 4], in1=TA[:, 12:L + 12],
                                                       op=ALU.add), TAb, TBb)
                    Sw, Swb = TBf, TBb
                dve(lambda e, Sw=Sw, L=L, gi=gi, s0=s0, wv=wv: e.scalar_tensor_tensor(
                    out=dB[:, gi, s0:s0 + L], in0=Sw[:, 8:8 + L], scalar=1.0 / wv, in1=Upad[:, 8:8 + L],
                    op0=ALU.mult, op1=ALU.subtract), Swb + UP, [pb_["dB"]])
                for (flag, which, c0) in ((first, 0, 0), (last, 1, L - 8)):
                    if not flag:
                        continue
                    dve(lambda e, Sw=Sw, c0=c0, which=which, gi=gi: e.tensor_tensor(
                        out=t8[:, 0:8], in0=Sw[:, 8 + c0:16 + c0], in1=edge[:, which, gi * 8:(gi + 1) * 8], op=ALU.mult),
                        Swb + [const_b], [pb_["t8"]])
                    dve(lambda e, c0=c0, gi=gi, s0=s0: e.tensor_tensor(
                        out=dB[:, gi, s0 + c0:s0 + c0 + 8], in0=t8[:, 0:8], in1=Upad[:, 8 + c0:16 + c0],
                        op=ALU.subtract), [pb_["t8"]] + UP, [pb_["dB"]])
        SZP, szpb = carve("SZP", O_GG, [4, S], F32)
        drain(bg["it"], 1)
        wt, wbf = w_in_piece(l, C_ZP)
        for gi in range(4):
            p1, b1 = bank()
            mm(p1[:], [(wt[:, kc, gi * 128:(gi + 1) * 128], hS(kc)) for kc in range(KC)], [wbf, hb], b1)
            act(SZP[:, gi, :], p1[:], AF.Silu, [b1], [szpb])
        Y, yb = carve("Y", O_Y, [KC, S], BF16)

        def P_final():
            for gi in range(4):
                p1, b1 = bank()
                mm(p1[:], [(wmap_s[:, gi, :], dB[:, gi, :])], [lay_b, pb_["dB"]], b1)
                dve(lambda e, gi=gi, p1=p1: e.scalar_tensor_tensor(out=Y[:, gi, :], in0=p1[:],
                                                                   scalar=VPT[:, l, 80 + gi:81 + gi],
                                                                   in1=SZP[:, gi, :], op0=ALU.mult, op1=ALU.mult),
                    [b1, vpt_b, szpb], [yb])

        NE = EXT if sample else S
        NBLK = NE // 128
        NPT = 10
        ab = {}
        o = O_T + 34816
        KT, ab["KT"] = carve("KT", o, [2, EXT], BF16); o += 2 * EXT * 2
        V, ab["V"] = carve("V", o, [6, 256], BF16); o += 6 * 256 * 2
        qsb = []
        for i in range(2):
            a_, ab["qsb%d" % i] = carve("qsb%d" % i, o, [512], F32)
            qsb.append(a_); o += 2048
        ab["t1"] = abuf("t1", o, 4096)
        t1 = AR.ap(o, [512], F32); t2 = AR.ap(o + 2048, [512], F32); o += 4096
        if sample:
            ab["rope"] = abuf("rope", o, 2 * EXT * 4, True)
            ropeC = AR.ap(o, [EXT], F32); ropeS = AR.ap(o + EXT * 4, [EXT], F32)
        else:
            kvs = []
            for i in range(2):
                a_, ab["kvs%d" % i] = carve("kvs%d" % i, o + i * 2048, [512], F32, True)
                kvs.append(a_)
        o += 2 * EXT * 4
        assert o <= ARENA_BYTES, o
        if sample:
            load(ropeC, c_cos[:, t0:t0 + EXT], [], ab["rope"])
            load(ropeS, c_sin[:, t0:t0 + EXT], [], ab["rope"])
        qi = [0]

        def roped_begin(p1, b1, n):
            q_, qb_ = qsb[qi[0] % 2], ab["qsb%d" % (qi[0] % 2)]
            qi[0] += 1
            act(q_[:, 0:n], p1[:, 0:n], AF.Copy, [b1], [qb_])
            return (q_, qb_, n)

        def roped_finish(st, a, out_ap, wbuf):
            q_, qb_, n = st
            p2, b2 = bank()
            mm(p2[:, 0:n], [(rot[:], q_[:, 0:n])], [const_b, qb_], b2)
            dve(lambda e: e.tensor_tensor(out=t1[:, 0:n], in0=q_[:, 0:n], in1=ropeC[:, a:a + n], op=ALU.mult),
                [qb_, ab["rope"]], [ab["t1"]])
            dve(lambda e: e.tensor_tensor(out=t2[:, 0:n], in0=p2[:, 0:n], in1=ropeS[:, a:a + n], op=ALU.mult),
                [b2, ab["rope"]], [ab["t1"]])
            dve(lambda e: e.tensor_tensor(out=out_ap, in0=t1[:, 0:n], in1=t2[:, 0:n], op=ALU.add), [ab["t1"]], [wbuf])

        def roped_seq(jobs):
            pend = None
            for (emit, n, a, out_ap, wbuf) in jobs:
                p1, b1 = emit()
                if not sample:
                    act(out_ap, p1[:, 0:n], AF.Copy, [b1], [wbuf])
                    continue
                st = roped_begin(p1, b1, n)
                if pend is not None:
                    roped_finish(*pend)
                pend = (st, a, out_ap, wbuf)
            if pend is not None:
                roped_finish(*pend)

        wkv, wkvb = w_in_piece(l, C_KV)
        jobs = []
        for kv in range(2):
            for (a, b) in ext_tiles:
                def emit(kv=kv, a=a, b=b):
                    p1, b1 = bank()
                    mm(p1[:, 0:b - a], [(wkv[:, kc, kv * 128:(kv + 1) * 128], hT[:, kc, a:b]) for kc in range(KC)],
                       [wkvb, hb], b1)
                    return p1, b1
                jobs.append((emit, b - a, a, KT[:, kv, a:b], ab["KT"]))
        roped_seq(jobs)
        for blk in range(NBLK):
            p1, b1 = bank()
            if sample:
                mm(p1[:, 0:256], [(hT[:, kc, blk * 128:(blk + 1) * 128], wkv[:, kc, 256:512]) for kc in range(KC)],
                   [wkvb, hb], b1)
                act(V[:, blk, :], p1[:, 0:256], AF.Copy, [b1], [ab["V"]])
            else:
                mm(p1[:], [(hT[:, kc, blk * 128:(blk + 1) * 128], wkv[:, kc, 0:512]) for kc in range(KC)], [wkvb, hb], b1)
                ks, ksb = kvs[blk % 2], ab["kvs%d" % (blk % 2)]
                act(ks, p1[:], AF.Copy, [b1], [ksb])
                dve(lambda e, ks=ks, blk=blk: e.tensor_copy(out=V[:, blk, :], in_=ks[:, 256:512]), [ksb], [ab["V"]])
                pidx = (g - NGS) * 2 + blk // 2
                r0 = (blk % 2) * 128
                P.dma("pool", nk[pidx, l, r0:r0 + 128, :], ks[:, 0:256], [ksb], [out_b], ksb.dsem)
                P.dma("pool", nv[pidx, l, r0:r0 + 128, :], ks[:, 256:512], [ksb], [out_b], ksb.dsem)
        P_final()
        drain(bg["it"], 1)
        o = O_T
        SZ, ab["SZ"] = carve("SZ", o, [4, S], F32); o += 4 * S * 4
        Eb = []
        for i in range(3):
            a_, ab["E%d" % i] = carve("E%d" % i, o, [512], BF16)
            Eb.append(a_); o += 1024
        PT = []
        for i in range(NPT):
            a_, ab["PT%d" % i] = carve("PT%d" % i, o, [512], BF16)
            PT.append(a_); o += 1024
        rd, ab["rd"] = carve("rd", o, [512], F32); o += 2048
        QT, ab["QT"] = carve("QT", o, [4, S], BF16); o += 4 * S * 2
        assert o <= O_T + 34816, o
        sc = float(128 ** -0.5)
        pti = [0]
        ei = [0]

        def unit_keys(kv, qb):
            keys = []
            if sample:
                gq = t0 // 128 + qb
                for dk in (-1, 0, 1):
                    if 0 <= gq + dk < T // 128:
                        eb = qb + 1 + dk
                        keys.append((KT[:, kv, eb * 128:(eb + 1) * 128], V[:, eb, kv * 128:(kv + 1) * 128],
                                     None if dk == 0 else (0 if dk < 0 else 1), [ab["KT"]], [ab["V"]]))
                for cbk in range(2):
                    keys.append((KTc[:, kv, cbk * 128:(cbk + 1) * 128], Vc[:, cbk, kv * 128:(kv + 1) * 128], None,
                                 [ctx_b], [ctx_b]))
            else:
                sg_ = qb // 2
                for kb in range(2):
                    eb = sg_ * 2 + kb
                    keys.append((KT[:, kv, eb * 128:(eb + 1) * 128], V[:, eb, kv * 128:(kv + 1) * 128], None,
                                 [ab["KT"]], [ab["V"]]))
            return keys

        def stage1(kv, qb):
            pts = []
            for (kT_ap, v_ap, mk, kr, vr) in unit_keys(kv, qb):
                p1, b1 = bank()
                mm(p1[:].rearrange("p (h q) -> p h q", q=128), [(kT_ap, QT[:, :, qb * 128:(qb + 1) * 128])],
                   kr + [ab["QT"]], b1)
                pt_, ptb = PT[pti[0] % NPT], ab["PT%d" % (pti[0] % NPT)]
                pti[0] += 1
                if mk is None:
                    act(pt_, p1[:], AF.Exp, [b1], [ptb], scale=sc)
                else:
                    e_, eb_ = Eb[ei[0] % 3], ab["E%d" % (ei[0] % 3)]
                    ei[0] += 1
                    act(e_, p1[:], AF.Exp, [b1], [eb_], scale=sc)
                    dve(lambda e, pt_=pt_, e_=e_, mk=mk: e.tensor_tensor(out=pt_, in0=e_, in1=masks[:, mk, :],
                                                                        op=ALU.mult), [eb_, const_b], [ptb])
                pts.append((pt_, ptb, v_ap, vr))
            return pts

        def stage2(kv, qb, pts):
            po, pob = bank()
            pd, pdb = bank()
            nkk = len(pts)
            for i, (pt_, ptb, v_ap, vr) in enumerate(pts):
                mm(po[:], [(v_ap, pt_)], vr + [ptb], pob, start=(i == 0), stop=(i == nkk - 1))
            for i, (pt_, ptb, v_ap, vr) in enumerate(pts):
                mm(pd[:], [(ones_b[:], pt_)], [const_b, ptb], pdb, start=(i == 0), stop=(i == nkk - 1))
            for h in range(4):
                dve(lambda e, h=h, pd=pd, kv=kv: e.tensor_scalar(
                    out=rd[:, h * 128:(h + 1) * 128], in0=pd[:, h * 128:(h + 1) * 128],
                    scalar1=es8[:, l, kv * 4 + h:kv * 4 + h + 1], scalar2=None, op0=ALU.add),
                    [pdb, vpt_b], [ab["rd"]])
            dve(lambda e: e.reciprocal(out=rd, in_=rd), [ab["rd"]], [ab["rd"]])
            dve(lambda e, po=po: e.tensor_tensor(out=rd, in0=po[:], in1=rd, op=ALU.mult), [pob, ab["rd"]], [ab["rd"]])
            dve(lambda e, kv=kv, qb=qb: e.tensor_tensor(
                out=Y[:, 4 + kv * 4:8 + kv * 4, qb * 128:(qb + 1) * 128],
                in0=rd.rearrange("p (h q) -> p h q", q=128), in1=SZ[:, :, qb * 128:(qb + 1) * 128], op=ALU.mult),
                [ab["rd"], ab["SZ"]], [yb])

        for kv in range(2):
            wz, wzb = w_in_piece(l, C_ZA + kv * 512)
            for h in range(4):
                p1, b1 = bank()
                mm(p1[:], [(wz[:, kc, h * 128:(h + 1) * 128], hS(kc)) for kc in range(KC)], [wzb, hb], b1)
                act(SZ[:, h, :], p1[:], AF.Silu, [b1], [ab["SZ"]])
            wq, wqb = w_in_piece(l, C_Q + kv * 512)
            jobs = []
            for h in range(4):
                def emit(h=h, wq=wq, wqb=wqb):
                    p1, b1 = bank()
                    mm(p1[:], [(wq[:, kc, h * 128:(h + 1) * 128], hS(kc)) for kc in range(KC)], [wqb, hb], b1)
                    return p1, b1
                jobs.append((emit, S, so, QT[:, h, :], ab["QT"]))
            roped_seq(jobs)
            prev = None
            for qb in range(4):
                cur = stage1(kv, qb)
                if prev is not None:
                    stage2(kv, qb - 1, prev)
                prev = cur
            stage2(kv, 3, prev)
            drain(bg["it"], 1)

        lb = {}
        o = O_T
        SZL, HSt = [], []
        for i in range(2):
            a_, lb["SZL%d" % i] = carve("SZL%d" % i, o + i * 2048, [S], F32)
            SZL.append(a_)
            a_, lb["HS%d" % i] = carve("HSt%d" % i, o + 4096 + i * 2048, [S], F32, True)
            HSt.append(a_)
        wz, wzb = w_in_piece(l, C_ZL)
        for n in range(4):
            p1, b1 = bank()
            mm(p1[:], [(wz[:, kc, n * 128:(n + 1) * 128], hS(kc)) for kc in range(KC)], [wzb, hb], b1)
            act(SZL[n % 2], p1[:], AF.Silu, [b1], [lb["SZL%d" % (n % 2)]])
            load(HSt[n % 2], HS_d[n, :, tb:tb + S], [HS_b[g]], lb["HS%d" % (n % 2)])
            dve(lambda e, n=n: e.tensor_tensor(out=Y[:, 12 + n, :], in0=SZL[n % 2], in1=HSt[n % 2], op=ALU.mult),
                [lb["SZL%d" % (n % 2)], lb["HS%d" % (n % 2)]], [yb])

        drain(bg["it"], 1)
        ada_compute_pending()
        MG, mgb = carve("MG", O_MG, [KC, S], BF16)
        gb_ = {}
        o = O_T + 8192
        ACC, gb_["ACC"] = carve("ACC", o, [4, S], F32); o += 4 * S * 4
        sg, tm = [], []
        for i in range(4):
            a_, gb_["sg%d" % i] = carve("sg%d" % i, o, [S], F32)
            sg.append(a_); o += 2048
        for i in range(2):
            a_, gb_["tm%d" % i] = carve("tm%d" % i, o, [S], F32)
            tm.append(a_); o += 2048
        sgi = [0]
        tmi = [0]
        branch = ((wb_po, "po", 0, 4), (wb_ao, "ao", 4, 8), (wb_lo, "lo", 12, 4))
        for jj in range(4):
            for b in range(3):
                wl, wlb = w_in_piece(l, C_MG + b * D + jj * 512)
                wsrc, wkey, y0, nkc = branch[b]
                wo, wob = wpiece(wsrc[l, :, jj * 512:(jj + 1) * 512].rearrange("(kc p) c -> p kc c", p=128),
                                 reads=[wconv[(wkey, l)]], nk=nkc)
                if b == 0:
                    drain(bg["it"], 1)
                for c in range(4):
                    j = jj * 4 + c
                    cs = slice(c * 128, (c + 1) * 128)
                    p1, b1 = bank()
                    mm(p1[:], [(wl[:, kc, cs], hS(kc)) for kc in range(KC)], [wlb, hb], b1)
                    s_, sb_ = sg[sgi[0] % 4], gb_["sg%d" % (sgi[0] % 4)]
                    sgi[0] += 1
                    act(s_, p1[:], AF.Sigmoid, [b1], [sb_])
                    p2, b2 = bank()
                    mm(p2[:], [(wo[:, kc, cs], Y[:, y0 + kc, :]) for kc in range(nkc)], [wob, yb], b2)
                    if b == 0:
                        dve(lambda e, c=c, s_=s_, p2=p2: e.tensor_tensor(out=ACC[:, c, :], in0=p2[:], in1=s_, op=ALU.mult),
                            [sb_, b2], [gb_["ACC"]])
                    else:
                        t_, tb_ = tm[tmi[0] % 2], gb_["tm%d" % (tmi[0] % 2)]
                        tmi[0] += 1
                        dve(lambda e, t_=t_, s_=s_, p2=p2: e.tensor_tensor(out=t_, in0=p2[:], in1=s_, op=ALU.mult),
                            [sb_, b2], [tb_])
                        if b == 1:
                            dve(lambda e, c=c, t_=t_: e.tensor_tensor(out=ACC[:, c, :], in0=ACC[:, c, :], in1=t_, op=ALU.add),
                                [tb_, gb_["ACC"]], [gb_["ACC"]])
                        else:
                            dve(lambda e, c=c, t_=t_, j=j: e.tensor_tensor(out=MG[:, j, :], in0=ACC[:, c, :], in1=t_,
                                                                           op=ALU.add), [tb_, gb_["ACC"]], [mgb])

        nxt = g + 1
        if nxt < NG:
            pre[(l, nxt)] = load_hT(l, nxt)
        ob = {}
        OUT, outb, XN = [], [], []
        for i in range(2):
            a_, b_ = carve("OUT%d" % i, O_Y + i * 8192, [D], F32)
            OUT.append(a_); outb.append(b_)
            XN.append(AR.ap(O_Y + i * 8192, [D], BF16))
        o = O_T + 24576
        junk, ob["junk"] = carve("junkO", o, [512], BF16); o += 1024
        ssq, ob["ss"] = carve("ssO", o, [32], F32); o += 128
        gtmp, ob["gt"] = carve("gtmp", o, [128], F32); o += 512
        gg, ob["gg"] = carve("gg", O_GG, [D], F32)
        xio = []
        for i in range(2):
            a_, ob["xio%d" % i] = carve("xio%d" % i, O_GG + 8192 + i * 8192, [D], F32, True)
            xio.append(a_)
        assert O_GG + 8192 + 16384 <= ARENA_BYTES
        fuse_n = l + 1 < DEPTH
        if fuse_n:
            hts, htsb = carve("hts", O_T + 51200, [KC, 128], BF16, True)
            ln = l + 1
            mnb = mod_bs[ln]
        for q in range(4):
            pt, pb2 = bank()
            for c4 in range(4):
                c = q * 4 + c4
                dve(lambda e, c=c: e.tensor_scalar_mul(out=gtmp, in0=ones_f[:], scalar1=Gmod[:, l, w, c:c + 1]),
                    [const_b, mod_b], [ob["gt"]])
                mm(pt[:, c4 * 128:(c4 + 1) * 128], [(gtmp, ident[:])], [ob["gt"], const_b], pb2)
            act(gg[:, q * 512:(q + 1) * 512], pt[:], AF.Copy, [pb2], [ob["gg"]])
        src, sbuf_d = x_src(l, g)
        dst, dbuf = x_dst(l, g)
        ws = [wpiece(wb_out[l, :, cg * 512:(cg + 1) * 512].rearrange("(kc p) c -> p kc c", p=128),
                     reads=[wconv[("out", l)]]) for cg in range(4)]

        def mmq(q):
            ot, otb = OUT[q % 2], outb[q % 2]
            for cg in range(4):
                wt, wbf = ws[cg]
                p1, b1 = bank()
                mm(p1[:], [(MG[:, kc, q * 128:(q + 1) * 128], wt[:, kc, :]) for kc in range(KC)], [mgb, wbf], b1)
                act(ot[:, cg * 512:(cg + 1) * 512], p1[:], AF.Copy, [b1], [otb])
                act(junk, p1[:], AF.Square, [b1], [ob["junk"], ob["ss"]],
                    accum_out=ssq[:, q * 4 + cg: q * 4 + cg + 1], scale=float(D ** -0.5))

        def chain_q(q):
            ot, otb = OUT[q % 2], outb[q % 2]
            xi, xb = xio[q % 2], ob["xio%d" % (q % 2)]
            load(xi, src[q * 128:(q + 1) * 128, :], [sbuf_d] if sbuf_d else [], xb)
            sl = ssq[:, q * 4:q * 4 + 4]
            r1 = ssq[:, 16 + q:17 + q]
            dve(lambda e: e.tensor_reduce(out=r1, in_=sl, axis=mybir.AxisListType.X, op=ALU.add), [ob["ss"]], [ob["ss"]])
            act(r1, r1, AF.Sqrt, [ob["ss"], const_b], [ob["ss"]], bias=small[:, 0:1])
            dve(lambda e: e.reciprocal(out=r1, in_=r1), [ob["ss"]], [ob["ss"]])
            dve(lambda e: e.scalar_tensor_tensor(out=ot, in0=ot, scalar=r1, in1=gg, op0=ALU.mult, op1=ALU.mult),
                [otb, ob["ss"], ob["gg"]], [otb])
            dve(lambda e: e.tensor_tensor(out=xi, in0=xi, in1=ot, op=ALU.add), [otb, xb], [xb])
            store(dst[q * 128:(q + 1) * 128, :], xi, xb, [dbuf])
            if fuse_n:
                r2 = ssq[:, 24 + q:25 + q]
                act(ot, xi, AF.Square, [xb], [otb, ob["ss"]], accum_out=ssq[:, 20 + q:21 + q], scale=float(D ** -0.5))
                act(r2, ssq[:, 20 + q:21 + q], AF.Sqrt, [ob["ss"], const_b], [ob["ss"]], bias=small[:, 0:1])
                dve(lambda e: e.reciprocal(out=r2, in_=r2), [ob["ss"]], [ob["ss"]])
                dve(lambda e: e.tensor_scalar_mul(out=XN[q % 2], in0=xi, scalar1=r2), [xb, ob["ss"]], [otb])

        def trans_q(q):
            if not fuse_n:
                return
            xn, otb = XN[q % 2], outb[q % 2]
            for qq in range(4):
                pt, pb2 = bank()
                ptb = pt[:].bitcast(BF16)
                for c4 in range(4):
                    c = qq * 4 + c4
                    tr(ptb[:, c4 * 128:(c4 + 1) * 128], xn[:, c * 128:(c + 1) * 128], ident_b[:], [otb, const_b], pb2)
                for c4 in range(4):
                    c = qq * 4 + c4
                    o_ = hts[:, c, :]
                    i_ = ptb[:, c4 * 128:(c4 + 1) * 128]
                    if c4 % 2 == 0:
                        act(o_, i_, AF.Identity, [pb2, mnb], [htsb], scale=Amod[:, ln, w, c:c + 1],
                            bias=modT[:, ln, c, w:w + 1])
                    else:
                        dve(lambda e, o_=o_, i_=i_, c=c: e.tensor_scalar(
                            out=o_, in0=i_, scalar1=Amod[:, ln, w, c:c + 1], scalar2=modT[:, ln, c, w:w + 1],
                            op0=ALU.mult, op1=ALU.add), [pb2, mnb], [htsb])
            store(hT_d[:, :, tb + q * 128: tb + (q + 1) * 128].rearrange("c p t -> p c t"), hts, htsb, [hT_b[g]])

        mmq(0); chain_q(0)
        drain(bg["it"], 1)
        mmq(1); chain_q(1)
        drain(bg["it"], 1)
        trans_q(0)
        mmq(2); chain_q(2)
        drain(bg["it"], 1)
        trans_q(1)
        mmq(3); chain_q(3)
        drain(bg["it"], 1)
        ada_issue_load()
        deferred.append(lambda: trans_q(2))
        deferred.append(lambda: trans_q(3))

    for l in range(DEPTH):
        layer(l)
    P.final_wait("sp")

    with nc.Block() as block:
        @block.tensor
        def _(e):
            P.replay("pe", e)

        @block.scalar
        def _(e):
            P.replay("act", e)

        @block.vector
        def _(e):
            P.replay("dve", e)

        @block.gpsimd
        def _(e):
            P.replay("pool", e)

        @block.sync
        def _(e):
            P.replay("sp", e)
    return nc


def _consts(T):
    ident = np.eye(128, dtype=np.float32)
    rot = np.zeros((128, 128), np.float32)
    for m in range(128):
        partner = m + 32 if (m % 64) < 32 else m - 32
        rot[partner, m] = 1.0
    j = np.arange(128)[:, None]
    i = np.arange(128)[None, :]
    mL = (i <= j).astype(np.float32)
    mU = (j <= i).astype(np.float32)
    mask = np.stack([np.tile(mL, (1, 4)), np.tile(mU, (1, 4))]).astype(ml_dtypes.bfloat16)
    GRID_W = 64
    t = np.arange(T)
    row = (t // GRID_W).astype(np.float32)
    col = (t % GRID_W).astype(np.float32)
    inv = (np.float32(10000.0) ** (-np.arange(32, dtype=np.float32) / np.float32(32))).astype(np.float32)
    ang_r = row[:, None] * inv[None]
    ang_c = col[:, None] * inv[None]
    cos = np.zeros((128, T + 2 * HALO), np.float32)
    sin = np.zeros((128, T + 2 * HALO), np.float32)
    for d in range(128):
        f = d % 32
        ang = ang_r[:, f] if d < 64 else ang_c[:, f]
        cos[d, HALO:HALO + T] = np.cos(ang)
        sgn = -1.0 if (d % 64) < 32 else 1.0
        sin[d, HALO:HALO + T] = sgn * np.sin(ang)
    edge = np.zeros((2, 32), np.float32)
    for gi, w in enumerate(POOL_W):
        hw = w // 2
        for c in range(8):
            edge[0, gi * 8 + c] = 1.0 / min(w, c + hw)
            edge[1, gi * 8 + c] = 1.0 / min(w, 8 - c + hw)
    return ident, rot, mask, cos, sin, edge


_CACHE = {}


def run(inputs, T, NP, DEPTH, n_cores):
    f = lambda a: np.ascontiguousarray(np.asarray(a, dtype=np.float32))
    key = (T, NP, DEPTH)
    if key not in _CACHE:
        _CACHE[key] = build_program(T, NP, DEPTH)
    nc = _CACHE[key]
    ident, rot, mask, cos, sin, edge = _consts(T)
    xp_, xs_ = f(inputs["x_prompt"]), f(inputs["x_sample"])
    ck_, cv_, st_ = f(inputs["cache_k"]), f(inputs["cache_v"]), f(inputs["state_lru"])
    c_, cctx = f(inputs["c"]), f(inputs["c_ctx"])
    vpk = np.concatenate([
        f(inputs["g_pre"]).reshape(DEPTH, 16, 128), f(inputs["g_post"]).reshape(DEPTH, 16, 128),
        f(inputs["b_ada"]).reshape(DEPTH, 48, 128), f(inputs["pool_scale"]).reshape(DEPTH, 4, 128),
        f(inputs["lru_conv_w"]).reshape(DEPTH, 16, 128), f(inputs["lru_conv_b"]).reshape(DEPTH, 4, 128),
        f(inputs["lru_gate_b"]).reshape(DEPTH, 16, 128), f(inputs["lru_lambda"]).reshape(DEPTH, 8, 128)], axis=1)
    shared = {
        "vp": np.ascontiguousarray(vpk), "sink": f(inputs["attn_sink"]),
        "w_ada": f(inputs["w_ada"]), "w_in": f(inputs["w_in"]), "w_map": f(inputs["w_pool_map"]),
        "w_gate": f(inputs["lru_gate_w"]).reshape(DEPTH, 16, 128, 128),
        "w_po": f(inputs["w_pool_o"]), "w_ao": f(inputs["w_attn_o"]), "w_lo": f(inputs["w_lru_o"]),
        "w_out": f(inputs["w_out"]),
        "c_ident": ident, "c_rot": rot, "c_mask": mask, "c_cos": cos, "c_sin": sin, "c_edge": edge,
    }
    in_maps = []
    for b in range(n_cores):
        c2 = np.stack([c_[b].reshape(16, 128), cctx.reshape(16, 128)], axis=1).reshape(32, 128)
        m = dict(shared)
        m.update({
            "xs": xs_[b], "xp": np.ascontiguousarray(xp_[b * NP:(b + 1) * NP].reshape(NP * SEQ, D)),
            "ck": np.ascontiguousarray(ck_[b].reshape(DEPTH, 256, 256)),
            "cv": np.ascontiguousarray(cv_[b].reshape(DEPTH, 256, 256)),
            "st": np.ascontiguousarray(st_[b].reshape(DEPTH * 8, 128)), "c2": np.ascontiguousarray(c2),
        })
        in_maps.append(m)
    res = run_bass_kernel_spmd(nc, in_maps, core_ids=list(range(n_cores)))
    R = res.results
    if DEBUG:
        _CACHE["last_dbg"] = {k: v for k, v in R[0].items() if k.startswith("dbg_")}
    y_p = np.concatenate([R[b]["yp"].reshape(NP, SEQ, D) for b in range(n_cores)], axis=0)
    y_s = np.stack([R[b]["ys"] for b in range(n_cores)], axis=0)
    n_k = np.concatenate([R[b]["nk"].reshape(NP, DEPTH, SEQ, 2, 128) for b in range(n_cores)], axis=0)
    n_v = np.concatenate([R[b]["nv"].reshape(NP, DEPTH, SEQ, 2, 128) for b in range(n_cores)], axis=0)
    n_h = np.concatenate([R[b]["nh"].reshape(NP, DEPTH, 2, 512) for b in range(n_cores)], axis=0)
    return (y_p.astype(np.float32), y_s.astype(np.float32), n_k.astype(np.float32), n_v.astype(np.float32),
            n_h.astype(np.float32))


def kernel(**inputs):
    return run(inputs, T=4096, NP=4, DEPTH=4, n_cores=8)
```

```python
import numpy as np
import ml_dtypes
import concourse.bass as bass
import concourse.mybir as mybir
from concourse.bass_utils import run_bass_kernel_spmd

F32 = mybir.dt.float32
BF16 = mybir.dt.bfloat16
ALU = mybir.AluOpType
AF = mybir.ActivationFunctionType

D = 2048
KC = 16
INW = 10752
SEQ = 256
C_UP, C_ZP, C_Q, C_KV, C_ZA, C_UL, C_ZL, C_MG = 0, 512, 1024, 2048, 2560, 3584, 4096, 4608
S = 512
HALO = 128
EXT = S + 2 * HALO
EPS = 1e-6
POOL_W = (2, 4, 8, 16)
SAME_ENGINE_SYNC = True
DEBUG = False


class Buf:
    __slots__ = ("name", "w", "r", "dsem")

    def __init__(self, name):
        self.name = name
        self.w = None
        self.r = {}
        self.dsem = None


class DSem:
    def __init__(self, sem):
        self.sem = sem
        self.count = 0


class Plan:
    ENG = ("pe", "act", "dve", "pool", "sp")

    def __init__(self, nc):
        self.nc = nc
        self.prog = {e: [] for e in self.ENG}
        self.sem = {e: nc.alloc_semaphore("prog_" + e) for e in self.ENG}
        self.cnt = {e: 0 for e in self.ENG}
        self.seen = {e: {} for e in self.ENG}
        self.dsems = []
        self.nbuf = 0
        self.named = {}
        self.pending = {}

    def buf(self, name="b"):
        self.nbuf += 1
        return Buf(name + str(self.nbuf))

    def new_dsem(self):
        d = DSem(self.nc.alloc_semaphore("dma%d" % len(self.dsems)))
        self.dsems.append(d)
        return d

    def slot(self, name="s"):
        b = self.buf(name)
        b.dsem = name
        return b

    def slot_dsem(self, name, eng):
        key = (name, "sw" if eng == "pool" else "hw")
        if key not in self.named:
            self.named[key] = self.new_dsem()
        return self.named[key]

    def _waits(self, eng, reads, writes):
        need = {}
        own = self.sem[eng]

        def add(tok):
            if tok is None:
                return
            sem, val = tok
            if sem is own and (eng == "pe" or not SAME_ENGINE_SYNC):
                return
            k = id(sem)
            if self.seen[eng].get(k, 0) >= val:
                return
            if k not in need or need[k][1] < val:
                need[k] = (sem, val)

        for b in reads:
            add(b.w)
        for b in writes:
            add(b.w)
            for t in b.r.values():
                add(t)
        return list(need.values())

    def _commit(self, eng, waits, fn, tok, inc, reads, writes):
        self.prog[eng].append((waits, fn, tok[0], inc))
        for sem, val in waits:
            self.seen[eng][id(sem)] = val
        for b in reads:
            k = id(tok[0])
            if k not in b.r or b.r[k][1] < tok[1]:
                b.r[k] = tok
        for b in writes:
            b.w = tok
            b.r = {}

    def op(self, eng, fn, reads=(), writes=()):
        waits = self._waits(eng, reads, writes)
        self.cnt[eng] += 1
        tok = (self.sem[eng], self.cnt[eng])
        self._commit(eng, waits, fn, tok, 1, reads, writes)
        return tok

    def dma(self, eng, out, in_, reads, writes, dsem, **kw):
        if isinstance(dsem, str):
            dsem = self.slot_dsem(dsem, eng)
        waits = self._waits(eng, reads, writes)
        dsem.count += 16
        tok = (dsem.sem, dsem.count)
        self._commit(eng, waits, lambda e: e.dma_start(out=out, in_=in_, **kw), tok, 16, reads, writes)
        return tok

    def inherit(self, old_bufs, new_bufs):
        toks = self.pending
        for b in old_bufs:
            for t in ([b.w] if b.w else []) + list(b.r.values()):
                k = id(t[0])
                if k not in toks or toks[k][1] < t[1]:
                    toks[k] = t
        for nb in new_bufs:
            for k, t in toks.items():
                if k not in nb.r or nb.r[k][1] < t[1]:
                    nb.r[k] = t

    def final_wait(self, eng):
        waits = []
        for d in self.dsems:
            if d.count > 0:
                waits.append((d.sem, d.count))
        for e in self.ENG:
            if e != eng and self.cnt[e] > 0:
                waits.append((self.sem[e], self.cnt[e]))
        self.prog[eng].append((waits, None, None, 0))

    def replay(self, eng, e):
        for waits, fn, sem, inc in self.prog[eng]:
            for s, v in waits:
                e.wait_ge(s, v)
            if fn is not None:
                ins = fn(e)
                ins.then_inc(sem, inc)


class Arena:
    def __init__(self, nc, name, nbytes):
        self.t = nc.alloc_sbuf_tensor(name, [128, nbytes // 4], F32)
        self.nbytes = nbytes

    def ap(self, off, shape, dtype):
        esz = 4 if dtype == F32 else 2
        n = int(np.prod(shape))
        assert off % 4 == 0 and off + n * esz <= self.nbytes, (off, shape, self.nbytes)
        nw = (n * esz + 3) // 4
        a = self.t[:, off // 4: off // 4 + nw]
        if dtype != F32:
            a = a.bitcast(dtype)
        if len(shape) == 2:
            a = a.rearrange("p (a b) -> p a b", b=shape[1])
        elif len(shape) == 3:
            a = a.rearrange("p (a b c) -> p a b c", b=shape[1], c=shape[2])
        return a


def build_program(T, NP, DEPTH):
    assert T % S == 0 and NP % 2 == 0
    TOKS = T + NP * SEQ
    NG = TOKS // S
    NGS = T // S
    nc = bass.Bass("TRN2", target_bir_lowering=False)
    P = Plan(nc)

    def din(name, shape, dt=F32):
        return nc.dram_tensor(name, list(shape), dt, kind="ExternalInput").ap()

    def dout(name, shape, dt=F32):
        return nc.dram_tensor(name, list(shape), dt, kind="ExternalOutput").ap()

    def dscr(name, shape, dt=F32):
        return nc.dram_tensor(name, list(shape), dt, kind="Internal").ap()

    xs = din("xs", [T, D])
    xp = din("xp", [NP * SEQ, D])
    ck = din("ck", [DEPTH, 256, 256])
    cv = din("cv", [DEPTH, 256, 256])
    st_in = din("st", [DEPTH * 8, 128])
    c2 = din("c2", [32, 128])
    vp = din("vp", [DEPTH, 128, 128])
    sink = din("sink", [DEPTH, 8])
    w_ada = din("w_ada", [DEPTH, D, 3 * D])
    w_in = din("w_in", [DEPTH, D, INW])
    w_map = din("w_map", [DEPTH, 4, 128, 128])
    w_gate = din("w_gate", [DEPTH, 16, 128, 128])
    w_po = din("w_po", [DEPTH, 512, D])
    w_ao = din("w_ao", [DEPTH, 1024, D])
    w_lo = din("w_lo", [DEPTH, 512, D])
    w_out = din("w_out", [DEPTH, D, D])
    c_ident = din("c_ident", [128, 128])
    c_rot = din("c_rot", [128, 128])
    c_mask = din("c_mask", [2, 128, 512], BF16)
    c_cos = din("c_cos", [128, T + 2 * HALO])
    c_sin = din("c_sin", [128, T + 2 * HALO])
    c_edge = din("c_edge", [2, 32])

    ys = dout("ys", [T, D])
    yp = dout("yp", [NP * SEQ, D])
    nk = dout("nk", [NP, DEPTH, SEQ, 256])
    nv = dout("nv", [NP, DEPTH, SEQ, 256])
    nh = dout("nh", [NP, DEPTH, 2, 512])

    xa_s = dscr("xa_s", [T, D])
    xa_p = dscr("xa_p", [NP * SEQ, D])
    hT_d = dscr("hT_d", [KC, 128, TOKS], BF16)
    HS_d = dscr("HS_d", [4, 128, TOKS])
    XC_d = dscr("XC_d", [4, 128, TOKS])
    wb_in = dscr("wb_in", [DEPTH, D, INW], BF16)
    wb_map = dscr("wb_map", [DEPTH, 4, 128, 128], BF16)
    wb_gate = dscr("wb_gate", [DEPTH, 16, 128, 128], BF16)
    wb_po = dscr("wb_po", [DEPTH, 512, D], BF16)
    wb_ao = dscr("wb_ao", [DEPTH, 1024, D], BF16)
    wb_lo = dscr("wb_lo", [DEPTH, 512, D], BF16)
    wb_out = dscr("wb_out", [DEPTH, D, D], BF16)

    hT_b = [P.buf("hT") for _ in range(NG)]
    HS_b = [P.buf("HS") for _ in range(NG)]
    XC_b = [P.buf("XC") for _ in range(NG)]
    xa_b = [P.buf("xa") for _ in range(NG)]
    y_b = [P.buf("y") for _ in range(NG)]
    out_b = P.buf("outs")
    wconv = {}

    def x_src(l, g):
        first_out_is_y = (DEPTH - 1) % 2 == 0
        if l == 0:
            src, b = None, None
        else:
            prev_is_y = ((DEPTH - 1 - (l - 1)) % 2 == 0)
            src, b = ("y", y_b[g]) if prev_is_y else ("xa", xa_b[g])
        if g < NGS:
            t0 = g * S
            tens = xs if l == 0 else (ys if src == "y" else xa_s)
        else:
            t0 = (g - NGS) * S
            tens = xp if l == 0 else (yp if src == "y" else xa_p)
        return tens[t0:t0 + S, :], b

    def x_dst(l, g):
        is_y = ((DEPTH - 1 - l) % 2 == 0)
        if g < NGS:
            t0 = g * S
            tens = ys if is_y else xa_s
        else:
            t0 = (g - NGS) * S
            tens = yp if is_y else xa_p
        return tens[t0:t0 + S, :], (y_b[g] if is_y else xa_b[g])

    WSL = 4
    wring = [nc.alloc_sbuf_tensor("wr%d" % i, [128, KC, 512], BF16) for i in range(WSL)]
    wring_b = [P.slot("wr%d" % i) for i in range(WSL)]
    wr_i = [0]

    def wslot():
        i = wr_i[0] % WSL
        wr_i[0] += 1
        return wring[i], wring_b[i]

    ident = nc.alloc_sbuf_tensor("ident", [128, 128], F32)
    rot = nc.alloc_sbuf_tensor("rot", [128, 128], F32)
    ident_b = nc.alloc_sbuf_tensor("ident_b", [128, 128], BF16)
    masks = nc.alloc_sbuf_tensor("masks", [128, 2, 512], BF16)
    ones_b = nc.alloc_sbuf_tensor("ones_b", [128, 128], BF16)
    ones_f = nc.alloc_sbuf_tensor("ones_f", [128, 128], F32)
    edge = nc.alloc_sbuf_tensor("edge", [128, 2, 32], F32)
    VPT = nc.alloc_sbuf_tensor("VPT", [128, DEPTH, 128], F32)
    H0T = nc.alloc_sbuf_tensor("H0T", [128, DEPTH * 8], F32)
    cTb = nc.alloc_sbuf_tensor("cTb", [128, 32], BF16)
    cTf = nc.alloc_sbuf_tensor("cTf", [128, 32], F32)
    modT = nc.alloc_sbuf_tensor("modT", [128, DEPTH, 48, 2], F32)
    Amod = nc.alloc_sbuf_tensor("Amod", [128, DEPTH, 2, 16], F32)
    Gmod = nc.alloc_sbuf_tensor("Gmod", [128, DEPTH, 2, 16], F32)
    spT = nc.alloc_sbuf_tensor("spT", [128, DEPTH, 8], F32)
    es8 = nc.alloc_sbuf_tensor("es8", [128, DEPTH, 8], F32)
    wmap_s = nc.alloc_sbuf_tensor("wmap_s", [128, 4, 128], BF16)
    wgate_s = nc.alloc_sbuf_tensor("wgate_s", [128, 16, 128], BF16)
    KTc = nc.alloc_sbuf_tensor("KTc", [128, 2, 256], BF16)
    Vc = nc.alloc_sbuf_tensor("Vc", [128, 2, 256], BF16)
    small = nc.alloc_sbuf_tensor("small", [128, 64], F32)
    const_b = P.buf("const")
    vpt_b = P.buf("vpt")
    lay_b = P.slot("layw")
    ctx_b = P.buf("ctx")
    small_b = P.buf("small")

    ARENA_BYTES = 110 * 1024
    AR = Arena(nc, "arena", ARENA_BYTES)
    arena_live = []

    def abuf(name, lo, nbytes, slot=False):
        hi = lo + nbytes
        nb = P.slot(name) if slot else P.buf(name)
        keep = []
        for (a, b, ob) in arena_live:
            if a < hi and lo < b:
                for t in ([ob.w] if ob.w else []) + list(ob.r.values()):
                    k = id(t[0])
                    if k not in nb.r or nb.r[k][1] < t[1]:
                        nb.r[k] = t
                if lo <= a and b <= hi:
                    continue
            keep.append((a, b, ob))
        arena_live[:] = keep + [(lo, hi, nb)]
        return nb

    def carve(name, off, shape, dtype, slot=False):
        esz = 4 if dtype == F32 else 2
        nbytes = (int(np.prod(shape)) * esz + 3) // 4 * 4
        return AR.ap(off, shape, dtype), abuf(name, off, nbytes, slot)

    psum = [nc.alloc_psum_tensor("ps%d" % i, [128, 512], F32) for i in range(8)]
    psum_b = [P.buf("ps") for _ in range(8)]
    ps_i = [0]

    def bank():
        i = ps_i[0] % 8
        ps_i[0] += 1
        return psum[i], psum_b[i]

    DBG = {}
    dbg_slot = P.slot("dbg")

    def dbg(name, ap, rbuf, n):
        if not DEBUG or name in DBG:
            return
        t = nc.dram_tensor("dbg_" + name, [128, n], F32, kind="ExternalOutput").ap()
        DBG[name] = t
        P.dma("sp", t, ap, [rbuf], [out_b], dbg_slot.dsem)

    def mm(out, pairs, reads, wbuf, start=True, stop=True):
        def fn(e):
            ins = None
            n = len(pairs)
            for i, (l, r) in enumerate(pairs):
                ins = e.matmul(out, lhsT=l, rhs=r, start=(start and i == 0), stop=(stop and i == n - 1))
            return ins
        return P.op("pe", fn, reads=reads, writes=[wbuf])

    def tr(out, in_, idn, reads, wbuf):
        return P.op("pe", lambda e: e.transpose(out=out, in_=in_, identity=idn), reads=reads, writes=[wbuf])

    def act(out, in_, func, reads, writes, **kw):
        return P.op("act", lambda e: e.activation(out=out, in_=in_, func=func, **kw), reads=reads, writes=writes)

    def dve(fn, reads, writes):
        return P.op("dve", fn, reads=reads, writes=writes)

    def load(out, in_, reads, wbuf, eng="sp", **kw):
        return P.dma(eng, out, in_, reads, [wbuf], wbuf.dsem, **kw)

    def store(out, in_, rbuf, wbufs, eng="pool", extra_reads=()):
        return P.dma(eng, out, in_, [rbuf] + list(extra_reads), wbufs, rbuf.dsem)

    def wpiece(dram_ap, reads=(), eng="sp", nk=KC, ncols=512):
        t, b = wslot()
        load(t[:, 0:nk, 0:ncols], dram_ap, list(reads), b, eng=eng)
        return t, b

    def w_in_piece(l, c0, ncols=512):
        return wpiece(wb_in[l, :, c0:c0 + ncols].rearrange("(kc p) c -> p kc c", p=128),
                      reads=[wconv[("in", l, in_group(c0))]], ncols=ncols)

    pro = {}
    vpin, pro["vpin"] = carve("vpin", 0, [128], F32, True)
    tmpa, pro["tmpa"] = carve("tmpa", 1024, [128], F32, True)
    tmpb, pro["tmpb"] = carve("tmpb", 2048, [128], F32)
    cslot = P.slot("cload")
    load(ident[:], c_ident, [], cslot)
    load(rot[:], c_rot, [], cslot)
    load(masks[:], c_mask.rearrange("m p c -> p m c"), [], cslot)
    load(edge[:].rearrange("p a b -> p (a b)"), c_edge.rearrange("a b -> (a b)").partition_broadcast(128), [], cslot)
    const_b.w = cslot.w
    dve(lambda e: e.tensor_copy(out=ident_b[:], in_=ident[:]), [const_b], [const_b])
    dve(lambda e: e.memset(ones_b[:], 1.0), [], [const_b])
    dve(lambda e: e.memset(ones_f[:], 1.0), [], [const_b])
    dve(lambda e: e.memset(small[:, 0:1], EPS), [], [const_b])
    dve(lambda e: e.memset(small[:, 1:2], 0.0), [], [const_b])
    dve(lambda e: e.memset(small[:, 2:3], 1.0), [], [const_b])

    def in_group(c0):
        return "A" if c0 == C_UL else ("C" if c0 >= C_MG else "B")

    def conv_gen(l):
        full = lambda key, dst, src, nrows, rows_per: (key, dst, src, nrows, rows_per)
        in_cols = [C_UL, C_UP, C_ZP, C_KV, C_ZA, C_Q, C_ZA + 512, C_Q + 512, C_ZL]
        mg_cols = [C_MG + b * D + jj * 512 for jj in range(4) for b in range(3)]
        inp = lambda c0: full(("in", l, in_group(c0)), wb_in[l][:, c0:c0 + 512], w_in[l][:, c0:c0 + 512], D, 512)
        specs = [full(("gate", l), wb_gate[l].rearrange("g c e -> (g c) e"), w_gate[l].rearrange("g c e -> (g c) e"), 2048, 1024),
                 full(("map", l), wb_map[l].rearrange("g c e -> (g c) e"), w_map[l].rearrange("g c e -> (g c) e"), 512, 512)]
        specs += [inp(c0) for c0 in in_cols]
        specs += [full(("po", l), wb_po[l], w_po[l], 512, 128), full(("ao", l), wb_ao[l], w_ao[l], 1024, 128),
                  full(("lo", l), wb_lo[l], w_lo[l], 512, 128)]
        specs += [inp(c0) for c0 in mg_cols]
        specs += [full(("out", l), wb_out[l], w_out[l], D, 128)]
        for key, dst, src, nrows, rows_per in specs:
            if key not in wconv:
                wconv[key] = P.buf("wc")
        for key, dst, src, nrows, rows_per in specs:
            ck_ = ("conv", key[0], key[2] if len(key) > 2 else -1, l % 2)
            if ck_ not in P.named:
                P.named[ck_] = P.new_dsem()
            ds = P.named[ck_]
            for r0 in range(0, nrows, rows_per):
                P.dma("pool", dst[r0:r0 + rows_per], src[r0:r0 + rows_per], [], [wconv[key]], ds)
                yield

    def drain(it, n=None):
        if it is None:
            return
        k = 0
        for _ in it:
            k += 1
            if n is not None and k >= n:
                return

    for l in range(DEPTH):
        load(vpin, vp[l], [], pro["vpin"])
        pt, pb = bank()
        tr(pt[:, 0:128], vpin, ident[:], [pro["vpin"], const_b], pb)
        act(VPT[:, l, :], pt[:, 0:128], AF.Copy, [pb], [vpt_b])
    load(vpin[0:DEPTH * 8, :], st_in, [], pro["vpin"])
    pt, pb = bank()
    tr(pt[:, 0:DEPTH * 8], vpin[0:DEPTH * 8, :], ident[0:DEPTH * 8, 0:DEPTH * 8], [pro["vpin"], const_b], pb)
    act(H0T[:], pt[:, 0:DEPTH * 8], AF.Copy, [pb], [vpt_b])
    load(vpin[0:32, :], c2, [], pro["vpin"])
    pt, pb = bank()
    tr(pt[:, 0:32], vpin[0:32, :], ident[0:32, 0:32], [pro["vpin"], const_b], pb)
    act(cTf[:], pt[:, 0:32], AF.Silu, [pb], [vpt_b])
    for l in range(DEPTH):
        act(tmpa[:, 0:8], VPT[:, l, 120:128], AF.Exp, [vpt_b], [pro["tmpa"]], scale=-1.0)
        act(tmpb[:, 0:8], tmpa[:, 0:8], AF.Ln, [pro["tmpa"], const_b], [pro["tmpb"]], bias=small[:, 2:3])
        if l == 0:
            dbg("lam", VPT[:, 0, 120:128], vpt_b, 8)
            dbg("tmpa", tmpa[:, 0:8], pro["tmpa"], 8)
            dbg("tmpb", tmpb[:, 0:8], pro["tmpb"], 8)
        dve(lambda e, l=l: e.tensor_scalar_mul(out=spT[:, l, :], in0=tmpb[:, 0:8], scalar1=-8.0), [pro["tmpb"]], [vpt_b])
        load(tmpa[:, 8:16], sink[l].partition_broadcast(128), [], pro["tmpa"], eng="sp")
        act(es8[:, l, :], tmpa[:, 8:16], AF.Exp, [pro["tmpa"]], [vpt_b])
        if l == 0:
            dbg("spT", spT[:, 0, :], vpt_b, 8)
            dbg("es8", es8[:, 0, :], vpt_b, 8)

    NADA = 24

    def ada_piece(l, pc):
        mb_ = mod_bs[l]
        t, b = wslot()
        t32 = t[:].rearrange("p a b -> p (a b)").bitcast(F32).rearrange("p (a b) -> p a b", b=256)
        load(t32, w_ada[l, :, pc * 256:(pc + 1) * 256].rearrange("(kc p) c -> p kc c", p=128), [], b, eng="sp")
        pt, pb = bank()
        for oc in range(2):
            mm(pt[:, oc * 2:oc * 2 + 2],
               [(t32[:, kc, oc * 128:(oc + 1) * 128], cTf[:, kc * 2:kc * 2 + 2]) for kc in range(KC)], [b, vpt_b], pb)
        for w in range(2):
            dve(lambda e, l=l, w=w, pt=pt, pc=pc: e.tensor_tensor(
                out=modT[:, l, pc * 2:(pc + 1) * 2, w], in0=pt[:, 0:4].rearrange("p (a b) -> p a b", b=2)[:, :, w],
                in1=VPT[:, l, 32 + pc * 2:34 + pc * 2], op=ALU.add), [pb, vpt_b], [mb_])
        if pc == NADA - 1:
            for w in range(2):
                dve(lambda e, l=l, w=w: e.scalar_tensor_tensor(out=Amod[:, l, w, :], in0=modT[:, l, 16:32, w], scalar=1.0,
                                                               in1=VPT[:, l, 0:16], op0=ALU.add, op1=ALU.mult),
                    [mb_, vpt_b], [mb_])
                dve(lambda e, l=l, w=w: e.tensor_tensor(out=Gmod[:, l, w, :], in0=modT[:, l, 32:48, w],
                                                        in1=VPT[:, l, 16:32], op=ALU.mult), [mb_, vpt_b], [mb_])

    def ada_gen(l):
        for pc in range(NADA):
            ada_piece(l, pc)
            yield

    def ada_piece_load(l, pc, slot_fn):
        raise NotImplementedError

    def ada_piece_compute(l, pc, t, b):
        raise NotImplementedError

    mod_bs = [P.buf("mod") for _ in range(DEPTH)]
    drain(ada_gen(0))
    drain(conv_gen(0))
    bg = {"it": None, "ada": None}

    def layer(l):
        load(wmap_s[:], wb_map[l].rearrange("g c e -> c g e"), [wconv[("map", l)]], lay_b)
        load(wgate_s[:], wb_gate[l].rearrange("g c e -> c g e"), [wconv[("gate", l)]], lay_b)
        cx = {}
        ckin, cx["ckin"] = carve("ckin", 0, [2, 256], F32, True)
        cvin, cx["cvin"] = carve("cvin", 2048, [2, 256], F32, True)
        load(ckin, ck[l].rearrange("(b p) c -> p b c", p=128), [], cx["ckin"])
        load(cvin, cv[l].rearrange("(b p) c -> p b c", p=128), [], cx["cvin"])
        pt, pb = bank()
        for kv in range(2):
            for cb in range(2):
                tr(pt[:, (kv * 2 + cb) * 128:(kv * 2 + cb + 1) * 128], ckin[:, cb, kv * 128:(kv + 1) * 128], ident[:],
                   [cx["ckin"], const_b], pb)
        act(KTc[:].rearrange("p a b -> p (a b)"), pt[:, 0:512], AF.Copy, [pb], [ctx_b])
        dve(lambda e: e.tensor_copy(out=Vc[:], in_=cvin), [cx["cvin"]], [ctx_b])

        def chain(*its):
            for it in its:
                for _ in it:
                    yield
        bg["ada"] = ada_gen(l + 1) if l + 1 < DEPTH else None
        ada_l = None
        ada_state["l"] = ada_l
        ada_state["next"] = 0
        ada_state["pending"] = None
        bg["it"] = conv_gen(l + 1) if l + 1 < DEPTH else None
        if l == 0:
            stage_N(l)
        drain(lru_stage(l, 0, T, S, None))
        for i in range(0, NP, 2):
            gens = [lru_stage(l, T + (i + j) * SEQ, SEQ, SEQ, i + j) for j in range(2)]
            live = list(gens)
            while live:
                for gq in list(live):
                    try:
                        next(gq)
                    except StopIteration:
                        live.remove(gq)
        drain(bg["ada"])
        for g in range(NG):
            main_tile(l, g)
        drain(bg["it"])
        ada_flush()
        for f in deferred:
            f()
        deferred.clear()

    def stage_N(l):
        sb = {}
        xin = [None, None]
        htt = [None, None]
        xin[0], sb["xin0"] = carve("xin0", 0, [D], F32, True)
        xin[1], sb["xin1"] = carve("xin1", 8192, [D], F32, True)
        junk, sb["junk"] = carve("junk", 16384, [D], BF16)
        htt[0], sb["ht0"] = carve("ht0", 20480, [KC, 512], BF16, True)
        htt[1], sb["ht1"] = carve("ht1", 20480 + 16384, [KC, 512], BF16, True)
        ssq, sb["ss"] = carve("ss", 20480 + 32768, [16], F32)
        it = 0
        for g in range(NG):
            w = 0 if g < NGS else 1
            src, sbuf_d = x_src(l, g)
            ht, htb = htt[g % 2], sb["ht%d" % (g % 2)]
            for j in range(4):
                xi, xb = xin[it % 2], sb["xin%d" % (it % 2)]
                col = it % 8
                it += 1
                load(xi, src[j * 128:(j + 1) * 128, :], [sbuf_d] if sbuf_d else [], xb)
                act(junk, xi, AF.Square, [xb], [sb["junk"], sb["ss"]], accum_out=ssq[:, col:col + 1],
                    scale=float(D ** -0.5))
                act(ssq[:, 8 + col:9 + col], ssq[:, col:col + 1], AF.Sqrt, [sb["ss"], const_b], [sb["ss"]],
                    bias=small[:, 0:1])
                dve(lambda e, col=col: e.reciprocal(out=ssq[:, 8 + col:9 + col], in_=ssq[:, 8 + col:9 + col]),
                    [sb["ss"]], [sb["ss"]])
                dve(lambda e, xi=xi, col=col: e.tensor_scalar_mul(out=xi, in0=xi, scalar1=ssq[:, 8 + col:9 + col]),
                    [xb, sb["ss"]], [xb])
                for q in range(4):
                    pt, pb = bank()
                    for c4 in range(4):
                        c = q * 4 + c4
                        tr(pt[:, c4 * 128:(c4 + 1) * 128], xi[:, c * 128:(c + 1) * 128], ident[:], [xb, const_b], pb)
                    for c4 in range(4):
                        c = q * 4 + c4
                        o = ht[:, c, j * 128:(j + 1) * 128]
                        i_ = pt[:, c4 * 128:(c4 + 1) * 128]
                        if c4 % 2 == 0:
                            act(o, i_, AF.Identity, [pb, mod_bs[l]], [htb], scale=Amod[:, l, w, c:c + 1],
                                bias=modT[:, l, c, w:w + 1])
                        else:
                            dve(lambda e, o=o, i_=i_, c=c, w=w: e.tensor_scalar(
                                out=o, in0=i_, scalar1=Amod[:, l, w, c:c + 1], scalar2=modT[:, l, c, w:w + 1],
                                op0=ALU.mult, op1=ALU.add), [pb, mod_bs[l]], [htb])
            store(hT_d[:, :, g * S:(g + 1) * S].rearrange("c p t -> p c t"), ht, htb, [hT_b[g]], eng="sp")

    def lru_stage(l, tb, TS, TL, pidx):
        nt = TS // TL
        W3 = TL + 3
        sb = {}
        o = 0 if pidx is None else (pidx % 2) * 56320
        hx, sb["hx"] = carve("hx_%d_%d" % (TL, o), o, [KC, W3 + 1], BF16, True); o_hx = o; o += KC * (W3 + 1) * 2
        U, sb["U"] = carve("U", o, [4, W3 + 1], F32); o += 4 * (W3 + 1) * 4
        XC = []
        for i in range(3):
            a_, sb["XC%d" % i] = carve("XC%d_%d_%d" % (i, TL, o), o, [4, TL], F32, True)
            XC.append(a_); o += 4 * TL * 4
        XCb, sb["XCb"] = carve("XCb", o, [4, TL], BF16); o += 4 * TL * 2
        HF = []
        for i in range(3):
            a_, sb["HF%d" % i] = carve("HF%d_%d_%d" % (i, TL, o), o, [4, TL], F32, True)
            HF.append(a_); o += 4 * TL * 4
        Gr, Gi, Ga, Gb = [], [], [], {}
        for n in range(4):
            a_, Gb[("r", n)] = carve("Gr%d" % n, o, [TL], F32); Gr.append(a_); o += TL * 4
            a_, Gb[("i", n)] = carve("Gi%d" % n, o, [TL], F32); Gi.append(a_); o += TL * 4
            a_, Gb[("a", n)] = carve("Ga%d" % n, o, [TL], F32); Ga.append(a_); o += TL * 4
        assert o <= ARENA_BYTES, o
        cw = lambda j, n: VPT[:, l, 84 + j * 4 + n: 85 + j * 4 + n]
        cb = lambda n: VPT[:, l, 100 + n:101 + n]
        gb = lambda d, k, n: VPT[:, l, 104 + (d * 2 + k) * 4 + n: 105 + (d * 2 + k) * 4 + n]

        def grp(t0):
            return (tb + t0) // S

        def gates_scan_all(d, xc, xcb_, outs, outb, inits, init_reads, rev):
            banks = []
            for n in range(4):
                p1, b1 = bank()
                mm(p1[:, 0:TL], [(wgate_s[:, (d * 2 + 0) * 4 + n, :], XCb[:, n, :])], [lay_b, sb["XCb"]], b1)
                p2, b2 = bank()
                mm(p2[:, 0:TL], [(wgate_s[:, (d * 2 + 1) * 4 + n, :], XCb[:, n, :])], [lay_b, sb["XCb"]], b2)
                banks.append((p1, b1, p2, b2))
            for n in range(4):
                p1, b1, p2, b2 = banks[n]
                act(Gr[n], p1[:, 0:TL], AF.Sigmoid, [b1, vpt_b], [Gb[("r", n)]], bias=gb(d, 0, n))
                act(Gi[n], p2[:, 0:TL], AF.Sigmoid, [b2, vpt_b], [Gb[("i", n)]], bias=gb(d, 1, n))
            for n in range(4):
                act(Ga[n], Gr[n], AF.Exp, [Gb[("r", n)], vpt_b], [Gb[("a", n)]], scale=spT[:, l, d * 4 + n: d * 4 + n + 1])
            for n in range(4):
                dve(lambda e, n=n: e.tensor_tensor(out=Gr[n], in0=Ga[n], in1=Ga[n], op=ALU.mult),
                    [Gb[("a", n)]], [Gb[("r", n)]])
                dve(lambda e, n=n: e.tensor_tensor(out=Gi[n], in0=Gi[n], in1=xc[:, n, :], op=ALU.mult),
                    [Gb[("i", n)], xcb_], [Gb[("i", n)]])
            for n in range(4):
                act(Gr[n], Gr[n], AF.Sqrt, [Gb[("r", n)], const_b], [Gb[("r", n)]], bias=small[:, 2:3], scale=-1.0)
            for n in range(4):
                dve(lambda e, n=n: e.tensor_tensor(out=Gi[n], in0=Gi[n], in1=Gr[n], op=ALU.mult),
                    [Gb[("i", n)], Gb[("r", n)]], [Gb[("i", n)]])
                o_ = outs[:, n, :]
                if rev:
                    dve(lambda e, n=n, o_=o_: e.tensor_tensor_scan(out=o_[:, ::-1], data0=Ga[n][:, ::-1],
                                                                   data1=Gi[n][:, ::-1], initial=inits[n],
                                                                   op0=ALU.mult, op1=ALU.add),
                        [Gb[("a", n)], Gb[("i", n)]] + init_reads, [outb])
                else:
                    dve(lambda e, n=n, o_=o_: e.tensor_tensor_scan(out=o_, data0=Ga[n], data1=Gi[n], initial=inits[n],
                                                                   op0=ALU.mult, op1=ALU.add),
                        [Gb[("a", n)], Gb[("i", n)]] + init_reads, [outb])

        single = (nt == 1)

        def phaseA(i):
            t0 = i * TL
            h, hb = hx, sb["hx"]
            lo = max(t0 - 2, 0)
            hi = min(t0 + TL + 1, TS)
            if t0 == 0:
                dve(lambda e, h=h: e.memset(h[:, :, 0:2], 0.0), [], [hb])
            if t0 + TL + 1 > TS:
                dve(lambda e, h=h: e.memset(h[:, :, TL + 2:TL + 3], 0.0), [], [hb])
            gs = sorted(set([grp(lo), grp(hi - 1)]))
            load(h[:, :, lo - (t0 - 2): hi - (t0 - 2)], hT_d[:, :, tb + lo: tb + hi].rearrange("c p t -> p c t"),
                 [hT_b[g] for g in gs], hb)
            xc, xcb_ = XC[i % 3], sb["XC%d" % (i % 3)]
            drain(bg["ada"], 2)
            wt, wbf = w_in_piece(l, C_UL)
            for n in range(4):
                p1, b1 = bank()
                mm(p1[:, 0:TL], [(wt[:, kc, n * 128:(n + 1) * 128], h[:, kc, 0:TL]) for kc in range(KC)], [wbf, hb], b1)
                p3, b3 = bank()
                mm(p3[:, 0:3], [(wt[:, kc, n * 128:(n + 1) * 128], h[:, kc, TL:TL + 3]) for kc in range(KC)],
                   [wbf, hb], b3)
                act(U[:, n, 0:TL], p1[:, 0:TL], AF.Copy, [b1], [sb["U"]])
                act(U[:, n, TL:TL + 3], p3[:, 0:3], AF.Copy, [b3], [sb["U"]])
                dve(lambda e, n=n, xc=xc: e.tensor_scalar(out=xc[:, n, :], in0=U[:, n, 0:TL], scalar1=cw(0, n),
                                                          scalar2=cb(n), op0=ALU.mult, op1=ALU.add),
                    [sb["U"], vpt_b], [xcb_])
                for j in range(1, 4):
                    dve(lambda e, n=n, xc=xc, j=j: e.scalar_tensor_tensor(out=xc[:, n, :], in0=U[:, n, j:j + TL],
                                                                          scalar=cw(j, n), in1=xc[:, n, :],
                                                                          op0=ALU.mult, op1=ALU.add),
                        [sb["U"], vpt_b], [xcb_])

        def phaseB(i):
            t0 = i * TL
            xc, xcb_ = XC[i % 3], sb["XC%d" % (i % 3)]
            act(XCb, xc, AF.Copy, [xcb_], [sb["XCb"]])
            if not single:
                store(XC_d[:, :, tb + t0: tb + t0 + TL].rearrange("n p t -> p n t"), xc, xcb_, [XC_b[grp(t0)]])
            hf, hfb = HF[i % 3], sb["HF%d" % (i % 3)]
            if i == 0:
                inits = [0.0 if pidx is not None else H0T[:, (l * 2 + 0) * 4 + n:(l * 2 + 0) * 4 + n + 1] for n in range(4)]
                ir = [vpt_b]
            else:
                inits = [HF[(i - 1) % 3][:, n, TL - 1:TL] for n in range(4)]
                ir = [sb["HF%d" % ((i - 1) % 3)]]
            gates_scan_all(0, xc, xcb_, hf, hfb, inits, ir, False)
            if not single:
                store(HS_d[:, :, tb + t0: tb + t0 + TL].rearrange("n p t -> p n t"), hf, hfb, [HS_b[grp(t0)]])
            if pidx is not None and i == nt - 1:
                for n in range(4):
                    P.dma("pool", nh[pidx, l, 0, n * 128:(n + 1) * 128].rearrange("(p o) -> p o", o=1),
                          hf[:, n, TL - 1:TL], [hfb], [out_b], hfb.dsem)

        phaseA(0)
        yield
        for i in range(nt):
            if i + 1 < nt:
                phaseA(i + 1)
            phaseB(i)
        yield
        HB = []
        for i in range(2):
            a_, sb["HB%d" % i] = carve("HB%d_%d_%d" % (i, TL, o_hx), o_hx + i * 4 * TL * 4, [4, TL], F32, True)
            HB.append(a_)
        for k in range(nt):
            i = nt - 1 - k
            t0 = i * TL
            if single:
                xc, xcb_ = XC[0], sb["XC0"]
                hf, hfb = HF[0], sb["HF0"]
            else:
                xc, xcb_ = XC[k % 3], sb["XC%d" % (k % 3)]
                hf, hfb = HF[k % 3], sb["HF%d" % (k % 3)]
                load(xc, XC_d[:, :, tb + t0: tb + t0 + TL].rearrange("n p t -> p n t"), [XC_b[grp(t0)]], xcb_)
                load(hf, HS_d[:, :, tb + t0: tb + t0 + TL].rearrange("n p t -> p n t"), [HS_b[grp(t0)]], hfb)
                act(XCb, xc, AF.Copy, [xcb_], [sb["XCb"]])
            hbk, hbb = HB[k % 2], sb["HB%d" % (k % 2)]
            if k == 0:
                inits = [0.0 if pidx is not None else H0T[:, (l * 2 + 1) * 4 + n:(l * 2 + 1) * 4 + n + 1] for n in range(4)]
                ir = [vpt_b]
            else:
                inits = [HB[(k - 1) % 2][:, n, 0:1] for n in range(4)]
                ir = [sb["HB%d" % ((k - 1) % 2)]]
            gates_scan_all(1, xc, xcb_, hbk, hbb, inits, ir, True)
            dve(lambda e, hf=hf, hbk=hbk: e.tensor_tensor(out=hf, in0=hf, in1=hbk, op=ALU.add), [hfb, hbb], [hfb])
            store(HS_d[:, :, tb + t0: tb + t0 + TL].rearrange("n p t -> p n t"), hf, hfb, [HS_b[grp(t0)]])
            if pidx is not None and i == 0:
                for n in range(4):
                    P.dma("pool", nh[pidx, l, 1, n * 128:(n + 1) * 128].rearrange("(p o) -> p o", o=1),
                          hbk[:, n, 0:1], [hbb], [out_b], hbb.dsem)

    O_HT, O_Y, O_MG, O_T = 0, 24576, 40960, 57344
    O_GG = O_T + 26624
    pre = {}
    deferred = []
    ada_state = {"l": None, "next": 0, "pending": None}

    def ada_compute_pending():
        if ada_state["pending"] is not None:
            pc, t, b = ada_state["pending"]
            ada_piece_compute(ada_state["l"], pc, t, b)
            ada_state["pending"] = None

    def ada_issue_load():
        if ada_state["l"] is None or ada_state["next"] >= 12:
            return
        pc = ada_state["next"]
        ada_state["next"] += 1
        t, b = ada_piece_load(ada_state["l"], pc, lambda: carve("adaw", O_MG, [KC, 512], BF16, True))
        ada_state["pending"] = (pc, t, b)

    def ada_flush():
        ada_compute_pending()
        while ada_state["l"] is not None and ada_state["next"] < 12:
            pc = ada_state["next"]
            ada_state["next"] += 1
            t, b = ada_piece_load(ada_state["l"], pc, wslot)
            ada_piece_compute(ada_state["l"], pc, t, b)

    def load_hT(l, g):
        sample = g < NGS
        hT, hb = carve("hT", O_HT, [KC, EXT], BF16, True)
        if sample:
            t0 = g * S
            lo, hi = max(t0 - HALO, 0), min(t0 + S + HALO, T)
            if t0 == 0:
                dve(lambda e: e.memset(hT[:, :, 0:HALO], 0.0), [], [hb])
            if t0 + S + HALO > T:
                dve(lambda e: e.memset(hT[:, :, HALO + S:EXT], 0.0), [], [hb])
            gs = sorted(set([lo // S, (hi - 1) // S]) | {g})
            load(hT[:, :, lo - (t0 - HALO): hi - (t0 - HALO)], hT_d[:, :, lo:hi].rearrange("c p t -> p c t"),
                 [hT_b[x] for x in gs], hb)
        else:
            load(hT[:, :, 0:S], hT_d[:, :, g * S:g * S + S].rearrange("c p t -> p c t"), [hT_b[g]], hb)
        return hT, hb

    def main_tile(l, g):
        sample = g < NGS
        w = 0 if sample else 1
        t0 = g * S if sample else (g - NGS) * S
        tb = g * S
        mod_b = mod_bs[l]
        if (l, g) in pre:
            hT, hb = pre.pop((l, g))
        else:
            hT, hb = load_hT(l, g)
        if sample:
            so = HALO
            ext_tiles = [(0, 512), (512, 768)]
            segs = [(0, S, t0 == 0, t0 + S == T)]
        else:
            so = 0
            ext_tiles = [(0, 512)]
            segs = [(0, SEQ, True, True), (SEQ, SEQ, True, True)]
        hS = lambda kc: hT[:, kc, so:so + S]

        pb_ = {}
        o = O_T
        Uext, pb_["Uext"] = carve("Uext", o, [4, EXT], F32); o += 4 * EXT * 4
        Upad, pb_["Upad"] = carve("Upad", o, [S + 16], F32); o += (S + 16) * 4
        TA, pb_["TA"] = carve("TA", o, [S + 16], F32); o += (S + 16) * 4
        TBf, pb_["TB"] = carve("TB", o, [S + 16], F32); o += (S + 16) * 4
        dB, pb_["dB"] = carve("dB", o, [4, S], BF16); o += 4 * S * 2
        t8, pb_["t8"] = carve("t8", o, [16], F32); o += 64
        assert o <= O_T + 24576
        wt, wbf = w_in_piece(l, C_UP)
        for (a, b) in ext_tiles:
            for gi in range(4):
                p1, b1 = bank()
                mm(p1[:, 0:b - a], [(wt[:, kc, gi * 128:(gi + 1) * 128], hT[:, kc, a:b]) for kc in range(KC)], [wbf, hb], b1)
                act(Uext[:, gi, a:b], p1[:, 0:b - a], AF.Copy, [b1], [pb_["Uext"]])
        for f in deferred:
            f()
        deferred.clear()
        for gi in range(4):
            wv = POOL_W[gi]
            for (s0, L, first, last) in segs:
                UP, TAb, TBb = [pb_["Upad"]], [pb_["TA"]], [pb_["TB"]]
                if sample:
                    dve(lambda e, gi=gi: e.tensor_copy(out=Upad[:, 0:S + 16], in_=Uext[:, gi, HALO - 8:HALO + S + 8]),
                        [pb_["Uext"]], UP)
                else:
                    dve(lambda e, L=L: e.memset(Upad[:, 0:8], 0.0), [], UP)
                    dve(lambda e, L=L: e.memset(Upad[:, 8 + L:16 + L], 0.0), [], UP)
                    dve(lambda e, gi=gi, L=L, s0=s0: e.tensor_copy(out=Upad[:, 8:8 + L], in_=Uext[:, gi, s0:s0 + L]),
                        [pb_["Uext"]], UP)
                dve(lambda e, L=L: e.tensor_tensor(out=TA[:, 1:L + 16], in0=Upad[:, 0:L + 15], in1=Upad[:, 1:L + 16],
                                                   op=ALU.add), UP, TAb)
                Sw, Swb = TA, TAb
                if wv >= 4:
                    dve(lambda e, L=L: e.tensor_tensor(out=TBf[:, 2:L + 14], in0=TA[:, 1:L + 13], in1=TA[:, 3:L + 15],
                                                       op=ALU.add), TAb, TBb)
                    Sw, Swb = TBf, TBb
                if wv >= 8:
                    dve(lambda e, L=L: e.tensor_tensor(out=TA[:, 4:L + 12], in0=TBf[:, 2:L + 10], in1=TBf[:, 6:L + 14],
                                                       op=ALU.add), TBb, TAb)
                    Sw, Swb = TA, TAb
                if wv >= 16:
                    dve(lambda e, L=L: e.tensor_tensor(out=TBf[:, 8:L + 8], in0=TA[:, 4:L + 4], in1=TA[:, 12:L + 12],
                                                       op=ALU.add), TAb, TBb)
                    Sw, Swb = TBf, TBb
                dve(lambda e, Sw=Sw, L=L, gi=gi, s0=s0, wv=wv: e.scalar_tensor_tensor(
                    out=dB[:, gi, s0:s0 + L], in0=Sw[:, 8:8 + L], scalar=1.0 / wv, in1=Upad[:, 8:8 + L],
                    op0=ALU.mult, op1=ALU.subtract), Swb + UP, [pb_["dB"]])
                for (flag, which, c0) in ((first, 0, 0), (last, 1, L - 8)):
                    if not flag:
                        continue
                    dve(lambda e, Sw=Sw, c0=c0, which=which, gi=gi: e.tensor_tensor(
                        out=t8[:, 0:8], in0=Sw[:, 8 + c0:16 + c0], in1=edge[:, which, gi * 8:(gi + 1) * 8], op=ALU.mult),
                        Swb + [const_b], [pb_["t8"]])
                    dve(lambda e, c0=c0, gi=gi, s0=s0: e.tensor_tensor(
                        out=dB[:, gi, s0 + c0:s0 + c0 + 8], in0=t8[:, 0:8], in1=Upad[:, 8 + c0:16 + c0],
                        op=ALU.subtract), [pb_["t8"]] + UP, [pb_["dB"]])
        SZP, szpb = carve("SZP", O_GG, [4, S], F32)
        drain(bg["it"], 1)
        wt, wbf = w_in_piece(l, C_ZP)
        for gi in range(4):
            p1, b1 = bank()
            mm(p1[:], [(wt[:, kc, gi * 128:(gi + 1) * 128], hS(kc)) for kc in range(KC)], [wbf, hb], b1)
            act(SZP[:, gi, :], p1[:], AF.Silu, [b1], [szpb])
        Y, yb = carve("Y", O_Y, [KC, S], BF16)

        def P_final():
            for gi in range(4):
                p1, b1 = bank()
                mm(p1[:], [(wmap_s[:, gi, :], dB[:, gi, :])], [lay_b, pb_["dB"]], b1)
                dve(lambda e, gi=gi, p1=p1: e.scalar_tensor_tensor(out=Y[:, gi, :], in0=p1[:],
                                                                   scalar=VPT[:, l, 80 + gi:81 + gi],
                                                                   in1=SZP[:, gi, :], op0=ALU.mult, op1=ALU.mult),
                    [b1, vpt_b, szpb], [yb])

        NE = EXT if sample else S
        NBLK = NE // 128
        NPT = 10
        ab = {}
        o = O_T + 34816
        KT, ab["KT"] = carve("KT", o, [2, EXT], BF16); o += 2 * EXT * 2
        V, ab["V"] = carve("V", o, [6, 256], BF16); o += 6 * 256 * 2
        qsb = []
        for i in range(2):
            a_, ab["qsb%d" % i] = carve("qsb%d" % i, o, [512], F32)
            qsb.append(a_); o += 2048
        ab["t1"] = abuf("t1", o, 4096)
        t1 = AR.ap(o, [512], F32); t2 = AR.ap(o + 2048, [512], F32); o += 4096
        if sample:
            ab["rope"] = abuf("rope", o, 2 * EXT * 4, True)
            ropeC = AR.ap(o, [EXT], F32); ropeS = AR.ap(o + EXT * 4, [EXT], F32)
        else:
            kvs = []
            for i in range(2):
                a_, ab["kvs%d" % i] = carve("kvs%d" % i, o + i * 2048, [512], F32, True)
                kvs.append(a_)
        o += 2 * EXT * 4
        assert o <= ARENA_BYTES, o
        if sample:
            load(ropeC, c_cos[:, t0:t0 + EXT], [], ab["rope"])
            load(ropeS, c_sin[:, t0:t0 + EXT], [], ab["rope"])
        qi = [0]

        def roped_begin(p1, b1, n):
            q_, qb_ = qsb[qi[0] % 2], ab["qsb%d" % (qi[0] % 2)]
            qi[0] += 1
            act(q_[:, 0:n], p1[:, 0:n], AF.Copy, [b1], [qb_])
            return (q_, qb_, n)

        def roped_finish(st, a, out_ap, wbuf):
            q_, qb_, n = st
            p2, b2 = bank()
            mm(p2[:, 0:n], [(rot[:], q_[:, 0:n])], [const_b, qb_], b2)
            dve(lambda e: e.tensor_tensor(out=t1[:, 0:n], in0=q_[:, 0:n], in1=ropeC[:, a:a + n], op=ALU.mult),
                [qb_, ab["rope"]], [ab["t1"]])
            dve(lambda e: e.tensor_tensor(out=t2[:, 0:n], in0=p2[:, 0:n], in1=ropeS[:, a:a + n], op=ALU.mult),
                [b2, ab["rope"]], [ab["t1"]])
            dve(lambda e: e.tensor_tensor(out=out_ap, in0=t1[:, 0:n], in1=t2[:, 0:n], op=ALU.add), [ab["t1"]], [wbuf])

        def roped_seq(jobs):
            pend = None
            for (emit, n, a, out_ap, wbuf) in jobs:
                p1, b1 = emit()
                if not sample:
                    act(out_ap, p1[:, 0:n], AF.Copy, [b1], [wbuf])
                    continue
                st = roped_begin(p1, b1, n)
                if pend is not None:
                    roped_finish(*pend)
                pend = (st, a, out_ap, wbuf)
            if pend is not None:
                roped_finish(*pend)

        wkv, wkvb = w_in_piece(l, C_KV)
        jobs = []
        for kv in range(2):
            for (a, b) in ext_tiles:
                def emit(kv=kv, a=a, b=b):
                    p1, b1 = bank()
                    mm(p1[:, 0:b - a], [(wkv[:, kc, kv * 128:(kv + 1) * 128], hT[:, kc, a:b]) for kc in range(KC)],
                       [wkvb, hb], b1)
                    return p1, b1
                jobs.append((emit, b - a, a, KT[:, kv, a:b], ab["KT"]))
        roped_seq(jobs)
        for blk in range(NBLK):
            p1, b1 = bank()
            if sample:
                mm(p1[:, 0:256], [(hT[:, kc, blk * 128:(blk + 1) * 128], wkv[:, kc, 256:512]) for kc in range(KC)],
                   [wkvb, hb], b1)
                act(V[:, blk, :], p1[:, 0:256], AF.Copy, [b1], [ab["V"]])
            else:
                mm(p1[:], [(hT[:, kc, blk * 128:(blk + 1) * 128], wkv[:, kc, 0:512]) for kc in range(KC)], [wkvb, hb], b1)
                ks, ksb = kvs[blk % 2], ab["kvs%d" % (blk % 2)]
                act(ks, p1[:], AF.Copy, [b1], [ksb])
                dve(lambda e, ks=ks, blk=blk: e.tensor_copy(out=V[:, blk, :], in_=ks[:, 256:512]), [ksb], [ab["V"]])
                pidx = (g - NGS) * 2 + blk // 2
                r0 = (blk % 2) * 128
                P.dma("pool", nk[pidx, l, r0:r0 + 128, :], ks[:, 0:256], [ksb], [out_b], ksb.dsem)
                P.dma("pool", nv[pidx, l, r0:r0 + 128, :], ks[:, 256:512], [ksb], [out_b], ksb.dsem)
        P_final()
        drain(bg["it"], 1)
        o = O_T
        SZ, ab["SZ"] = carve("SZ", o, [4, S], F32); o += 4 * S * 4
        Eb = []
        for i in range(3):
            a_, ab["E%d" % i] = carve("E%d" % i, o, [512], BF16)
            Eb.append(a_); o += 1024
        PT = []
        for i in range(NPT):
            a_, ab["PT%d" % i] = carve("PT%d" % i, o, [512], BF16)
            PT.append(a_); o += 1024
        rd, ab["rd"] = carve("rd", o, [512], F32); o += 2048
        QT, ab["QT"] = carve("QT", o, [4, S], BF16); o += 4 * S * 2
        assert o <= O_T + 34816, o
        sc = float(128 ** -0.5)
        pti = [0]
        ei = [0]

        def unit_keys(kv, qb):
            keys = []
            if sample:
                gq = t0 // 128 + qb
                for dk in (-1, 0, 1):
                    if 0 <= gq + dk < T // 128:
                        eb = qb + 1 + dk
                        keys.append((KT[:, kv, eb * 128:(eb + 1) * 128], V[:, eb, kv * 128:(kv + 1) * 128],
                                     None if dk == 0 else (0 if dk < 0 else 1), [ab["KT"]], [ab["V"]]))
                for cbk in range(2):
                    keys.append((KTc[:, kv, cbk * 128:(cbk + 1) * 128], Vc[:, cbk, kv * 128:(kv + 1) * 128], None,
                                 [ctx_b], [ctx_b]))
            else:
                sg_ = qb // 2
                for kb in range(2):
                    eb = sg_ * 2 + kb
                    keys.append((KT[:, kv, eb * 128:(eb + 1) * 128], V[:, eb, kv * 128:(kv + 1) * 128], None,
                                 [ab["KT"]], [ab["V"]]))
            return keys

        def stage1(kv, qb):
            pts = []
            for (kT_ap, v_ap, mk, kr, vr) in unit_keys(kv, qb):
                p1, b1 = bank()
                mm(p1[:].rearrange("p (h q) -> p h q", q=128), [(kT_ap, QT[:, :, qb * 128:(qb + 1) * 128])],
                   kr + [ab["QT"]], b1)
                pt_, ptb = PT[pti[0] % NPT], ab["PT%d" % (pti[0] % NPT)]
                pti[0] += 1
                if mk is None:
                    act(pt_, p1[:], AF.Exp, [b1], [ptb], scale=sc)
                else:
                    e_, eb_ = Eb[ei[0] % 3], ab["E%d" % (ei[0] % 3)]
                    ei[0] += 1
                    act(e_, p1[:], AF.Exp, [b1], [eb_], scale=sc)
                    dve(lambda e, pt_=pt_, e_=e_, mk=mk: e.tensor_tensor(out=pt_, in0=e_, in1=masks[:, mk, :],
                                                                        op=ALU.mult), [eb_, const_b], [ptb])
                pts.append((pt_, ptb, v_ap, vr))
            return pts

        def stage2(kv, qb, pts):
            po, pob = bank()
            pd, pdb = bank()
            nkk = len(pts)
            for i, (pt_, ptb, v_ap, vr) in enumerate(pts):
                mm(po[:], [(v_ap, pt_)], vr + [ptb], pob, start=(i == 0), stop=(i == nkk - 1))
            for i, (pt_, ptb, v_ap, vr) in enumerate(pts):
                mm(pd[:], [(ones_b[:], pt_)], [const_b, ptb], pdb, start=(i == 0), stop=(i == nkk - 1))
            for h in range(4):
                dve(lambda e, h=h, pd=pd, kv=kv: e.tensor_scalar(
                    out=rd[:, h * 128:(h + 1) * 128], in0=pd[:, h * 128:(h + 1) * 128],
                    scalar1=es8[:, l, kv * 4 + h:kv * 4 + h + 1], scalar2=None, op0=ALU.add),
                    [pdb, vpt_b], [ab["rd"]])
            dve(lambda e: e.reciprocal(out=rd, in_=rd), [ab["rd"]], [ab["rd"]])
            dve(lambda e, po=po: e.tensor_tensor(out=rd, in0=po[:], in1=rd, op=ALU.mult), [pob, ab["rd"]], [ab["rd"]])
            dve(lambda e, kv=kv, qb=qb: e.tensor_tensor(
                out=Y[:, 4 + kv * 4:8 + kv * 4, qb * 128:(qb + 1) * 128],
                in0=rd.rearrange("p (h q) -> p h q", q=128), in1=SZ[:, :, qb * 128:(qb + 1) * 128], op=ALU.mult),
                [ab["rd"], ab["SZ"]], [yb])

        for kv in range(2):
            wz, wzb = w_in_piece(l, C_ZA + kv * 512)
            for h in range(4):
                p1, b1 = bank()
                mm(p1[:], [(wz[:, kc, h * 128:(h + 1) * 128], hS(kc)) for kc in range(KC)], [wzb, hb], b1)
                act(SZ[:, h, :], p1[:], AF.Silu, [b1], [ab["SZ"]])
            wq, wqb = w_in_piece(l, C_Q + kv * 512)
            jobs = []
            for h in range(4):
                def emit(h=h, wq=wq, wqb=wqb):
                    p1, b1 = bank()
                    mm(p1[:], [(wq[:, kc, h * 128:(h + 1) * 128], hS(kc)) for kc in range(KC)], [wqb, hb], b1)
                    return p1, b1
                jobs.append((emit, S, so, QT[:, h, :], ab["QT"]))
            roped_seq(jobs)
            prev = None
            for qb in range(4):
                cur = stage1(kv, qb)
                if prev is not None:
                    stage2(kv, qb - 1, prev)
                prev = cur
            stage2(kv, 3, prev)
            drain(bg["it"], 1)

        lb = {}
        o = O_T
        SZL, HSt = [], []
        for i in range(2):
            a_, lb["SZL%d" % i] = carve("SZL%d" % i, o + i * 2048, [S], F32)
            SZL.append(a_)
            a_, lb["HS%d" % i] = carve("HSt%d" % i, o + 4096 + i * 2048, [S], F32, True)
            HSt.append(a_)
        wz, wzb = w_in_piece(l, C_ZL)
        for n in range(4):
            p1, b1 = bank()
            mm(p1[:], [(wz[:, kc, n * 128:(n + 1) * 128], hS(kc)) for kc in range(KC)], [wzb, hb], b1)
            act(SZL[n % 2], p1[:], AF.Silu, [b1], [lb["SZL%d" % (n % 2)]])
            load(HSt[n % 2], HS_d[n, :, tb:tb + S], [HS_b[g]], lb["HS%d" % (n % 2)])
            dve(lambda e, n=n: e.tensor_tensor(out=Y[:, 12 + n, :], in0=SZL[n % 2], in1=HSt[n % 2], op=ALU.mult),
                [lb["SZL%d" % (n % 2)], lb["HS%d" % (n % 2)]], [yb])

        drain(bg["it"], 1)
        ada_compute_pending()
        MG, mgb = carve("MG", O_MG, [KC, S], BF16)
        gb_ = {}
        o = O_T + 8192
        ACC, gb_["ACC"] = carve("ACC", o, [4, S], F32); o += 4 * S * 4
        sg, tm = [], []
        for i in range(4):
            a_, gb_["sg%d" % i] = carve("sg%d" % i, o, [S], F32)
            sg.append(a_); o += 2048
        for i in range(2):
            a_, gb_["tm%d" % i] = carve("tm%d" % i, o, [S], F32)
            tm.append(a_); o += 2048
        sgi = [0]
        tmi = [0]
        branch = ((wb_po, "po", 0, 4), (wb_ao, "ao", 4, 8), (wb_lo, "lo", 12, 4))
        for jj in range(4):
            for b in range(3):
                wl, wlb = w_in_piece(l, C_MG + b * D + jj * 512)
                wsrc, wkey, y0, nkc = branch[b]
                wo, wob = wpiece(wsrc[l, :, jj * 512:(jj + 1) * 512].rearrange("(kc p) c -> p kc c", p=128),
                                 reads=[wconv[(wkey, l)]], nk=nkc)
                if b == 0:
                    drain(bg["it"], 1)
                for c in range(4):
                    j = jj * 4 + c
                    cs = slice(c * 128, (c + 1) * 128)
                    p1, b1 = bank()
                    mm(p1[:], [(wl[:, kc, cs], hS(kc)) for kc in range(KC)], [wlb, hb], b1)
                    s_, sb_ = sg[sgi[0] % 4], gb_["sg%d" % (sgi[0] % 4)]
                    sgi[0] += 1
                    act(s_, p1[:], AF.Sigmoid, [b1], [sb_])
                    p2, b2 = bank()
                    mm(p2[:], [(wo[:, kc, cs], Y[:, y0 + kc, :]) for kc in range(nkc)], [wob, yb], b2)
                    if b == 0:
                        dve(lambda e, c=c, s_=s_, p2=p2: e.tensor_tensor(out=ACC[:, c, :], in0=p2[:], in1=s_, op=ALU.mult),
                            [sb_, b2], [gb_["ACC"]])
                    else:
                        t_, tb_ = tm[tmi[0] % 2], gb_["tm%d" % (tmi[0] % 2)]
                        tmi[0] += 1
                        dve(lambda e, t_=t_, s_=s_, p2=p2: e.tensor_tensor(out=t_, in0=p2[:], in1=s_, op=ALU.mult),
                            [sb_, b2], [tb_])
                        if b == 1:
                            dve(lambda e, c=c, t_=t_: e.tensor_tensor(out=ACC[:, c, :], in0=ACC[:, c, :], in1=t_, op=ALU.add),
                                [tb_, gb_["ACC"]], [gb_["ACC"]])
                        else:
                            dve(lambda e, c=c, t_=t_, j=j: e.tensor_tensor(out=MG[:, j, :], in0=ACC[:, c, :], in1=t_,
                                                                           op=ALU.add), [tb_, gb_["ACC"]], [mgb])

        nxt = g + 1
        if nxt < NG:
            pre[(l, nxt)] = load_hT(l, nxt)
        ob = {}
        OUT, outb, XN = [], [], []
        for i in range(2):
            a_, b_ = carve("OUT%d" % i, O_Y + i * 8192, [D], F32)
            OUT.append(a_); outb.append(b_)
            XN.append(AR.ap(O_Y + i * 8192, [D], BF16))
        o = O_T + 24576
        junk, ob["junk"] = carve("junkO", o, [512], BF16); o += 1024
        ssq, ob["ss"] = carve("ssO", o, [32], F32); o += 128
        gtmp, ob["gt"] = carve("gtmp", o, [128], F32); o += 512
        gg, ob["gg"] = carve("gg", O_GG, [D], F32)
        xio = []
        for i in range(2):
            a_, ob["xio%d" % i] = carve("xio%d" % i, O_GG + 8192 + i * 8192, [D], F32, True)
            xio.append(a_)
        assert O_GG + 8192 + 16384 <= ARENA_BYTES
        fuse_n = l + 1 < DEPTH
        if fuse_n:
            hts, htsb = carve("hts", O_T + 51200, [KC, 128], BF16, True)
            ln = l + 1
            mnb = mod_bs[ln]
        for q in range(4):
            pt, pb2 = bank()
            for c4 in range(4):
                c = q * 4 + c4
                dve(lambda e, c=c: e.tensor_scalar_mul(out=gtmp, in0=ones_f[:], scalar1=Gmod[:, l, w, c:c + 1]),
                    [const_b, mod_b], [ob["gt"]])
                mm(pt[:, c4 * 128:(c4 + 1) * 128], [(gtmp, ident[:])], [ob["gt"], const_b], pb2)
            act(gg[:, q * 512:(q + 1) * 512], pt[:], AF.Copy, [pb2], [ob["gg"]])
        src, sbuf_d = x_src(l, g)
        dst, dbuf = x_dst(l, g)
        ws = [wpiece(wb_out[l, :, cg * 512:(cg + 1) * 512].rearrange("(kc p) c -> p kc c", p=128),
                     reads=[wconv[("out", l)]]) for cg in range(4)]

        def mmq(q):
            ot, otb = OUT[q % 2], outb[q % 2]
            for cg in range(4):
                wt, wbf = ws[cg]
                p1, b1 = bank()
                mm(p1[:], [(MG[:, kc, q * 128:(q + 1) * 128], wt[:, kc, :]) for kc in range(KC)], [mgb, wbf], b1)
                act(ot[:, cg * 512:(cg + 1) * 512], p1[:], AF.Copy, [b1], [otb])
                act(junk, p1[:], AF.Square, [b1], [ob["junk"], ob["ss"]],
                    accum_out=ssq[:, q * 4 + cg: q * 4 + cg + 1], scale=float(D ** -0.5))

        def chain_q(q):
            ot, otb = OUT[q % 2], outb[q % 2]
            xi, xb = xio[q % 2], ob["xio%d" % (q % 2)]
            load(xi, src[q * 128:(q + 1) * 128, :], [sbuf_d] if sbuf_d else [], xb)
            sl = ssq[:, q * 4:q * 4 + 4]
            r1 = ssq[:, 16 + q:17 + q]
            dve(lambda e: e.tensor_reduce(out=r1, in_=sl, axis=mybir.AxisListType.X, op=ALU.add), [ob["ss"]], [ob["ss"]])
            act(r1, r1, AF.Sqrt, [ob["ss"], const_b], [ob["ss"]], bias=small[:, 0:1])
            dve(lambda e: e.reciprocal(out=r1, in_=r1), [ob["ss"]], [ob["ss"]])
            dve(lambda e: e.scalar_tensor_tensor(out=ot, in0=ot, scalar=r1, in1=gg, op0=ALU.mult, op1=ALU.mult),
                [otb, ob["ss"], ob["gg"]], [otb])
            dve(lambda e: e.tensor_tensor(out=xi, in0=xi, in1=ot, op=ALU.add), [otb, xb], [xb])
            store(dst[q * 128:(q + 1) * 128, :], xi, xb, [dbuf])
            if fuse_n:
                r2 = ssq[:, 24 + q:25 + q]
                act(ot, xi, AF.Square, [xb], [otb, ob["ss"]], accum_out=ssq[:, 20 + q:21 + q], scale=float(D ** -0.5))
                act(r2, ssq[:, 20 + q:21 + q], AF.Sqrt, [ob["ss"], const_b], [ob["ss"]], bias=small[:, 0:1])
                dve(lambda e: e.reciprocal(out=r2, in_=r2), [ob["ss"]], [ob["ss"]])
                dve(lambda e: e.tensor_scalar_mul(out=XN[q % 2], in0=xi, scalar1=r2), [xb, ob["ss"]], [otb])

        def trans_q(q):
            if not fuse_n:
                return
            xn, otb = XN[q % 2], outb[q % 2]
            for qq in range(4):
                pt, pb2 = bank()
                ptb = pt[:].bitcast(BF16)
                for c4 in range(4):
                    c = qq * 4 + c4
                    tr(ptb[:, c4 * 128:(c4 + 1) * 128], xn[:, c * 128:(c + 1) * 128], ident_b[:], [otb, const_b], pb2)
                for c4 in range(4):
                    c = qq * 4 + c4
                    o_ = hts[:, c, :]
                    i_ = ptb[:, c4 * 128:(c4 + 1) * 128]
                    if c4 % 2 == 0:
                        act(o_, i_, AF.Identity, [pb2, mnb], [htsb], scale=Amod[:, ln, w, c:c + 1],
                            bias=modT[:, ln, c, w:w + 1])
                    else:
                        dve(lambda e, o_=o_, i_=i_, c=c: e.tensor_scalar(
                            out=o_, in0=i_, scalar1=Amod[:, ln, w, c:c + 1], scalar2=modT[:, ln, c, w:w + 1],
                            op0=ALU.mult, op1=ALU.add), [pb2, mnb], [htsb])
            store(hT_d[:, :, tb + q * 128: tb + (q + 1) * 128].rearrange("c p t -> p c t"), hts, htsb, [hT_b[g]])

        mmq(0); chain_q(0)
        drain(bg["it"], 1)
        mmq(1); chain_q(1)
        drain(bg["it"], 1)
        trans_q(0)
        mmq(2); chain_q(2)
        drain(bg["it"], 1)
        trans_q(1)
        mmq(3); chain_q(3)
        drain(bg["it"], 1)
        ada_issue_load()
        deferred.append(lambda: trans_q(2))
        deferred.append(lambda: trans_q(3))

    for l in range(DEPTH):
        layer(l)
    P.final_wait("sp")

    with nc.Block() as block:
        @block.tensor
        def _(e):
            P.replay("pe", e)

        @block.scalar
        def _(e):
            P.replay("act", e)

        @block.vector
        def _(e):
            P.replay("dve", e)

        @block.gpsimd
        def _(e):
            P.replay("pool", e)

        @block.sync
        def _(e):
            P.replay("sp", e)
    return nc


def _consts(T):
    ident = np.eye(128, dtype=np.float32)
    rot = np.zeros((128, 128), np.float32)
    for m in range(128):
        partner = m + 32 if (m % 64) < 32 else m - 32
        rot[partner, m] = 1.0
    j = np.arange(128)[:, None]
    i = np.arange(128)[None, :]
    mL = (i <= j).astype(np.float32)
    mU = (j <= i).astype(np.float32)
    mask = np.stack([np.tile(mL, (1, 4)), np.tile(mU, (1, 4))]).astype(ml_dtypes.bfloat16)
    GRID_W = 64
    t = np.arange(T)
    row = (t // GRID_W).astype(np.float32)
    col = (t % GRID_W).astype(np.float32)
    inv = (np.float32(10000.0) ** (-np.arange(32, dtype=np.float32) / np.float32(32))).astype(np.float32)
    ang_r = row[:, None] * inv[None]
    ang_c = col[:, None] * inv[None]
    cos = np.zeros((128, T + 2 * HALO), np.float32)
    sin = np.zeros((128, T + 2 * HALO), np.float32)
    for d in range(128):
        f = d % 32
        ang = ang_r[:, f] if d < 64 else ang_c[:, f]
        cos[d, HALO:HALO + T] = np.cos(ang)
        sgn = -1.0 if (d % 64) < 32 else 1.0
        sin[d, HALO:HALO + T] = sgn * np.sin(ang)
    edge = np.zeros((2, 32), np.float32)
    for gi, w in enumerate(POOL_W):
        hw = w // 2
        for c in range(8):
            edge[0, gi * 8 + c] = 1.0 / min(w, c + hw)
            edge[1, gi * 8 + c] = 1.0 / min(w, 8 - c + hw)
    return ident, rot, mask, cos, sin, edge


_CACHE = {}


def run(inputs, T, NP, DEPTH, n_cores):
    f = lambda a: np.ascontiguousarray(np.asarray(a, dtype=np.float32))
    key = (T, NP, DEPTH)
    if key not in _CACHE:
        _CACHE[key] = build_program(T, NP, DEPTH)
    nc = _CACHE[key]
    ident, rot, mask, cos, sin, edge = _consts(T)
    xp_, xs_ = f(inputs["x_prompt"]), f(inputs["x_sample"])
    ck_, cv_, st_ = f(inputs["cache_k"]), f(inputs["cache_v"]), f(inputs["state_lru"])
    c_, cctx = f(inputs["c"]), f(inputs["c_ctx"])
    vpk = np.concatenate([
        f(inputs["g_pre"]).reshape(DEPTH, 16, 128), f(inputs["g_post"]).reshape(DEPTH, 16, 128),
        f(inputs["b_ada"]).reshape(DEPTH, 48, 128), f(inputs["pool_scale"]).reshape(DEPTH, 4, 128),
        f(inputs["lru_conv_w"]).reshape(DEPTH, 16, 128), f(inputs["lru_conv_b"]).reshape(DEPTH, 4, 128),
        f(inputs["lru_gate_b"]).reshape(DEPTH, 16, 128), f(inputs["lru_lambda"]).reshape(DEPTH, 8, 128)], axis=1)
    shared = {
        "vp": np.ascontiguousarray(vpk), "sink": f(inputs["attn_sink"]),
        "w_ada": f(inputs["w_ada"]), "w_in": f(inputs["w_in"]), "w_map": f(inputs["w_pool_map"]),
        "w_gate": f(inputs["lru_gate_w"]).reshape(DEPTH, 16, 128, 128),
        "w_po": f(inputs["w_pool_o"]), "w_ao": f(inputs["w_attn_o"]), "w_lo": f(inputs["w_lru_o"]),
        "w_out": f(inputs["w_out"]),
        "c_ident": ident, "c_rot": rot, "c_mask": mask, "c_cos": cos, "c_sin": sin, "c_edge": edge,
    }
    in_maps = []
    for b in range(n_cores):
        c2 = np.stack([c_[b].reshape(16, 128), cctx.reshape(16, 128)], axis=1).reshape(32, 128)
        m = dict(shared)
        m.update({
            "xs": xs_[b], "xp": np.ascontiguousarray(xp_[b * NP:(b + 1) * NP].reshape(NP * SEQ, D)),
            "ck": np.ascontiguousarray(ck_[b].reshape(DEPTH, 256, 256)),
            "cv": np.ascontiguousarray(cv_[b].reshape(DEPTH, 256, 256)),
            "st": np.ascontiguousarray(st_[b].reshape(DEPTH * 8, 128)), "c2": np.ascontiguousarray(c2),
        })
        in_maps.append(m)
    res = run_bass_kernel_spmd(nc, in_maps, core_ids=list(range(n_cores)))
    R = res.results
    if DEBUG:
        _CACHE["last_dbg"] = {k: v for k, v in R[0].items() if k.startswith("dbg_")}
    y_p = np.concatenate([R[b]["yp"].reshape(NP, SEQ, D) for b in range(n_cores)], axis=0)
    y_s = np.stack([R[b]["ys"] for b in range(n_cores)], axis=0)
    n_k = np.concatenate([R[b]["nk"].reshape(NP, DEPTH, SEQ, 2, 128) for b in range(n_cores)], axis=0)
    n_v = np.concatenate([R[b]["nv"].reshape(NP, DEPTH, SEQ, 2, 128) for b in range(n_cores)], axis=0)
    n_h = np.concatenate([R[b]["nh"].reshape(NP, DEPTH, 2, 512) for b in range(n_cores)], axis=0)
    return (y_p.astype(np.float32), y_s.astype(np.float32), n_k.astype(np.float32), n_v.astype(np.float32),
            n_h.astype(np.float32))


def kernel(**inputs):
    return run(inputs, T=4096, NP=4, DEPTH=4, n_cores=8)
```
